# Optimizing a Trainium2 kernel written in Bass

```python
import math
import jax, jax.numpy as jnp
from jax import lax
import numpy as np

D_MODEL = 1024
BATCH = 8
SEQ = 2048
DEPTH = 1

CHUNK = 64
N_META = 16
Q_BLOCK = 128
N_HEADS = 16
QK_NOPE = 64
QK_ROPE = 32
V_DIM = 64
Q_RANK = 256
KV_RANK = 128
ROPE_BASE = 10000.0
CONV_CH = D_MODEL
CONV_K = 31
D_FF = ((8 * D_MODEL + 3 * 256 - 1) // (3 * 256)) * 256
EPS = 1e-6
NEG_INF = -1e30

IN_Q = Q_RANK
IN_KV = KV_RANK
IN_KR = QK_ROPE
IN_GLU = 2 * CONV_CH
IN_GATE = 2 * D_MODEL
OFF_KV = IN_Q
OFF_KR = OFF_KV + IN_KV
OFF_GLU = OFF_KR + IN_KR
OFF_GATE = OFF_GLU + IN_GLU
N_IN = OFF_GATE + IN_GATE

kernel_name = "hybrid_mla_conformer_conv_swiglu_block"


def _rms_norm(x, g):
    xf = x.astype(jnp.float32)
    y = xf * lax.rsqrt(jnp.mean(xf * xf, axis=-1, keepdims=True) + EPS)
    return (y * g.astype(jnp.float32)).astype(x.dtype)


def _layer_norm(x, g, b):
    xf = x.astype(jnp.float32)
    mu = jnp.mean(xf, axis=-1, keepdims=True)
    xc = xf - mu
    var = jnp.mean(xc * xc, axis=-1, keepdims=True)
    y = xc * lax.rsqrt(var + EPS) * g.astype(jnp.float32) + b.astype(jnp.float32)
    return y.astype(x.dtype)


def _rope_tables(length):
    pos = jnp.arange(length, dtype=jnp.float32)
    inv = ROPE_BASE ** (-jnp.arange(0, QK_ROPE, 2, dtype=jnp.float32) / QK_ROPE)
    ang = pos[:, None] * inv[None, :]
    return jnp.cos(ang), jnp.sin(ang)


def _apply_rope(x, cos, sin):
    half = QK_ROPE // 2
    xf = x.astype(jnp.float32)
    x1, x2 = xf[..., :half], xf[..., half:]
    out = jnp.concatenate([x1 * cos - x2 * sin, x2 * cos + x1 * sin], axis=-1)
    return out.astype(x.dtype)


def _chunk_end(pos):
    if pos < N_META:
        return N_META
    return N_META + CHUNK * ((pos - N_META) // CHUNK + 1)


def _mla_branch(c_q, c_kv, k_r, q_norm_g, w_uq, kv_norm_g, w_ukv, w_attn_o):
    b, length, _ = c_q.shape
    q = _rms_norm(c_q, q_norm_g) @ w_uq
    q = q.reshape(b, length, N_HEADS, QK_NOPE + QK_ROPE)
    q_nope, q_rope = q[..., :QK_NOPE], q[..., QK_NOPE:]
    kv = _rms_norm(c_kv, kv_norm_g) @ w_ukv
    kv = kv.reshape(b, length, N_HEADS, QK_NOPE + V_DIM)
    k_nope, v = kv[..., :QK_NOPE], kv[..., QK_NOPE:]
    cos, sin = _rope_tables(length)
    q_rope = _apply_rope(q_rope, cos[None, :, None, :], sin[None, :, None, :])
    k_rope = _apply_rope(k_r, cos[None], sin[None])

    pos = jnp.arange(length)
    cid = jnp.where(pos < N_META, 0, 1 + (pos - N_META) // CHUNK)
    scale = 1.0 / math.sqrt(QK_NOPE + QK_ROPE)
    outs = []
    for s in range(0, length, Q_BLOCK):
        e = min(s + Q_BLOCK, length)
        kend = min(_chunk_end(e - 1), length)
        sc = (jnp.einsum('bqhd,bkhd->bhqk', q_nope[:, s:e], k_nope[:, :kend])
              + jnp.einsum('bqhr,bkr->bhqk', q_rope[:, s:e], k_rope[:, :kend]))
        sc = sc.astype(jnp.float32) * scale
        mask = cid[None, :kend] <= cid[s:e, None]
        sc = jnp.where(mask[None, None], sc, NEG_INF)
        p = jax.nn.softmax(sc, axis=-1).astype(v.dtype)
        outs.append(jnp.einsum('bhqk,bkhd->bqhd', p, v[:, :kend]))
    o = jnp.concatenate(outs, axis=1).reshape(b, length, N_HEADS * V_DIM)
    return o @ w_attn_o


def _conv_branch(glu_in, conv_w, conv_b, conv_ln_g, conv_ln_b, w_conv_o):
    a, g = glu_in[..., :CONV_CH], glu_in[..., CONV_CH:]
    u = a * jax.nn.sigmoid(g)
    y = lax.conv_general_dilated(
        u, conv_w[:, None, :].astype(u.dtype), window_strides=(1,),
        padding=[(CONV_K - 1, 0)], dimension_numbers=('NWC', 'WIO', 'NWC'),
        feature_group_count=CONV_CH)
    y = y + conv_b
    y = jax.nn.silu(_layer_norm(y, conv_ln_g, conv_ln_b))
    return y @ w_conv_o


def _layer(x, mix_pre_g, w_in, q_norm_g, w_uq, kv_norm_g, w_ukv, w_attn_o,
           conv_w, conv_b, conv_ln_g, conv_ln_b, w_conv_o, w_out, mix_post_g,
           ffn_pre_g, w_ffn_in, w_ffn_out, ffn_post_g):
    h = _rms_norm(x, mix_pre_g)
    proj = h @ w_in
    c_q = proj[..., :OFF_KV]
    c_kv = proj[..., OFF_KV:OFF_KR]
    k_r = proj[..., OFF_KR:OFF_GLU]
    glu_in = proj[..., OFF_GLU:OFF_GATE]
    g_a = proj[..., OFF_GATE:OFF_GATE + D_MODEL]
    g_b = proj[..., OFF_GATE + D_MODEL:]
    y_a = _mla_branch(c_q, c_kv, k_r, q_norm_g, w_uq, kv_norm_g, w_ukv, w_attn_o)
    y_b = _conv_branch(glu_in, conv_w, conv_b, conv_ln_g, conv_ln_b, w_conv_o)
    merged = jax.nn.sigmoid(g_a) * y_a + jax.nn.sigmoid(g_b) * y_b
    x = x + _rms_norm(merged @ w_out, mix_post_g)
    h = _rms_norm(x, ffn_pre_g)
    gu = h @ w_ffn_in
    f = (jax.nn.silu(gu[..., :D_FF]) * gu[..., D_FF:]) @ w_ffn_out
    return x + _rms_norm(f, ffn_post_g)


def setup_inputs(seed: int = 0) -> dict:
    key = jax.random.key(seed)
    ks = jax.random.split(key, 24)
    f32 = jnp.float32

    def w(k, shape, fan_in):
        return jax.random.normal(k, shape, f32) * (fan_in ** -0.5)

    def gain(k, n):
        return 1.0 + 0.02 * jax.random.normal(k, (DEPTH, n), f32)

    def bias(k, n):
        return 0.02 * jax.random.normal(k, (DEPTH, n), f32)

    return {
        "x": jax.random.normal(ks[0], (BATCH, SEQ, D_MODEL), f32),
        "meta": jax.random.normal(ks[1], (N_META, D_MODEL), f32),
        "mix_pre_g": gain(ks[2], D_MODEL),
        "w_in": w(ks[3], (DEPTH, D_MODEL, N_IN), D_MODEL),
        "q_norm_g": gain(ks[4], Q_RANK),
        "w_uq": w(ks[5], (DEPTH, Q_RANK, N_HEADS * (QK_NOPE + QK_ROPE)), Q_RANK),
        "kv_norm_g": gain(ks[6], KV_RANK),
        "w_ukv": w(ks[7], (DEPTH, KV_RANK, N_HEADS * (QK_NOPE + V_DIM)), KV_RANK),
        "w_attn_o": w(ks[8], (DEPTH, N_HEADS * V_DIM, D_MODEL), N_HEADS * V_DIM),
        "conv_w": w(ks[9], (DEPTH, CONV_K, CONV_CH), CONV_K),
        "conv_b": bias(ks[10], CONV_CH),
        "conv_ln_g": gain(ks[11], CONV_CH),
        "conv_ln_b": bias(ks[12], CONV_CH),
        "w_conv_o": w(ks[13], (DEPTH, CONV_CH, D_MODEL), CONV_CH),
        "w_out": w(ks[14], (DEPTH, D_MODEL, D_MODEL), D_MODEL),
        "mix_post_g": gain(ks[15], D_MODEL),
        "ffn_pre_g": gain(ks[16], D_MODEL),
        "w_ffn_in": w(ks[17], (DEPTH, D_MODEL, 2 * D_FF), D_MODEL),
        "w_ffn_out": w(ks[18], (DEPTH, D_FF, D_MODEL), D_FF),
        "ffn_post_g": gain(ks[19], D_MODEL),
    }


def reference(x, meta, mix_pre_g, w_in, q_norm_g, w_uq, kv_norm_g, w_ukv,
              w_attn_o, conv_w, conv_b, conv_ln_g, conv_ln_b, w_conv_o, w_out,
              mix_post_g, ffn_pre_g, w_ffn_in, w_ffn_out, ffn_post_g):
    b = x.shape[0]
    m = jnp.broadcast_to(meta.astype(x.dtype)[None], (b, N_META, D_MODEL))
    h = jnp.concatenate([m, x], axis=1)
    for l in range(DEPTH):
        h = _layer(h, mix_pre_g[l], w_in[l], q_norm_g[l], w_uq[l], kv_norm_g[l],
                   w_ukv[l], w_attn_o[l], conv_w[l], conv_b[l], conv_ln_g[l],
                   conv_ln_b[l], w_conv_o[l], w_out[l], mix_post_g[l],
                   ffn_pre_g[l], w_ffn_in[l], w_ffn_out[l], ffn_post_g[l])
    return h[:, N_META:]
```

```python
import os
from contextlib import ExitStack

import numpy as np
import ml_dtypes

import concourse.bass as bass
import concourse.mybir as mybir
from concourse.bass_utils import run_bass_kernel_spmd

F32 = mybir.dt.float32
BF16 = mybir.dt.bfloat16
AF = mybir.ActivationFunctionType
ALU = mybir.AluOpType
AX = mybir.AxisListType

D = 1024
SEQ = 2048
NMETA = 16
NCOL = SEQ + NMETA
NH = 16
EPS = 1e-6
DFF = 2816
NFC = DFF // 128
OFF_KV = 256
OFF_KR = 384
OFF_GLU = 416
OFF_GATE = 2464
N_IN = 4512
CONV_K = 31
SCALE = 1.0 / float(np.sqrt(96.0))

C_PRE = 0
C_QG = 8
C_KVG = 10
C_CB = 11
C_LNG = 19
C_LNB = 27
C_FPRE = 35
C_CW = 43
NCOLS = C_CW + 8 * CONV_K

DEBUG = os.environ.get("MK_DEBUG", "") != ""


class Prog:
    def __init__(self):
        self.ops = []
        self.lw = {}
        self.rd = {}
        self.floor = None

    def op(self, eng, fn, r=(), w=(), dma=None):
        oid = len(self.ops)
        raw = set()
        war = set()
        for k in r:
            if k in self.lw:
                raw.add(self.lw[k])
        for k in w:
            if k in self.lw:
                raw.add(self.lw[k])
            war.update(self.rd.get(k, ()))
        if self.floor is not None:
            raw.add(self.floor)
        for k in r:
            self.rd.setdefault(k, []).append(oid)
        for k in w:
            self.lw[k] = oid
            self.rd[k] = []
        deps = set()
        for d in raw | war:
            dop = self.ops[d]
            if dop["dma"] is not None:
                deps.add(d)
            elif dop["eng"] != eng:
                deps.add(d)
            else:
                if eng != "tensor":
                    deps.add(d)
        self.ops.append(dict(eng=eng, fn=fn, deps=deps, dma=dma, has_dep=False))
        return oid

    def barrier(self, eng="vector", fn=None):
        oid = len(self.ops)
        deps = set(self.lw.values())
        for v in self.rd.values():
            deps.update(v)
        if self.floor is not None:
            deps.add(self.floor)
        self.ops.append(dict(eng=eng, fn=fn, deps=deps, dma=None, has_dep=False))
        self.floor = oid
        self.lw = {}
        self.rd = {}
        return oid

    def emit(self, nc, stack):
        ops = self.ops
        for o in ops:
            for d in o["deps"]:
                ops[d]["has_dep"] = True
        engs = ["tensor", "vector", "scalar", "gpsimd", "sync"]
        esem = {e: stack.enter_context(nc.semaphore("s_" + e)) for e in engs}
        ecnt = {e: 0 for e in engs}
        dsem = {}
        dcnt = {}
        for o in ops:
            if not o["has_dep"]:
                o["sig"] = None
                continue
            if o["dma"] is not None:
                k = o["dma"]
                if k not in dsem:
                    dsem[k] = stack.enter_context(nc.semaphore("d_%d" % len(dsem)))
                    dcnt[k] = 0
                dcnt[k] += 16
                o["sig"] = (dsem[k], dcnt[k], 16)
            else:
                assert o["fn"] is not None
                ecnt[o["eng"]] += 1
                o["sig"] = (esem[o["eng"]], ecnt[o["eng"]], 1)
        per = {e: [] for e in engs}
        for o in ops:
            per[o["eng"]].append(o)
        block = stack.enter_context(nc.Block())

        def make(e_name):
            def body(e):
                waited = {}
                for o in per[e_name]:
                    need = {}
                    for d in o["deps"]:
                        sem, val, _ = ops[d]["sig"]
                        key = id(sem)
                        if waited.get(key, 0) >= val:
                            continue
                        if key not in need or need[key][1] < val:
                            need[key] = (sem, val)
                    for key, (sem, val) in need.items():
                        e.wait_ge(sem, val)
                        waited[key] = val
                    if o["fn"] is None:
                        continue
                    ins = o["fn"](e)
                    if o["sig"] is not None:
                        ins.then_inc(o["sig"][0], o["sig"][2])
            return body

        block.tensor(make("tensor"))
        block.vector(make("vector"))
        block.scalar(make("scalar"))
        block.gpsimd(make("gpsimd"))
        block.sync(make("sync"))
        return ecnt, len(dsem)


def build_nc():
    nc = bass.Bass("TRN2", target_bir_lowering=False)
    P = Prog()

    def din(name, shape, dt=F32):
        return nc.dram_tensor(name, list(shape), dt, kind="ExternalInput").ap()

    x = din("x", [SEQ, D])
    meta = din("meta", [NMETA, D])
    w_in = din("w_in", [D, N_IN])
    w_uq = din("w_uq", [256, 1536])
    w_ukv = din("w_ukv", [128, 2048])
    w_ao = din("w_attn_o", [D, D])
    w_co = din("w_conv_o", [D, D])
    w_out = din("w_out", [D, D])
    w_fi = din("w_ffn_in", [D, 2 * DFF])
    w_fo = din("w_ffn_out", [DFF, D])
    cols_d = din("cols", [128, NCOLS])
    gpost_d = din("gpost", [128, 2, D])
    cmat_d = din("cmat", [128, 4, 128], BF16)
    trig_d = din("trig", [128, 2, NCOL])
    y = nc.dram_tensor("y", [SEQ, D], F32, kind="ExternalOutput").ap()
    dbg = {}
    if DEBUG:
        dbg["hT"] = nc.dram_tensor("dbg_hT", [128, 8, NCOL], BF16, kind="ExternalOutput").ap()
        dbg["lat"] = nc.dram_tensor("dbg_lat", [128, 4, NCOL], BF16, kind="ExternalOutput").ap()
        dbg["OT"] = nc.dram_tensor("dbg_OT", [128, 8, SEQ], BF16, kind="ExternalOutput").ap()
        dbg["maT"] = nc.dram_tensor("dbg_maT", [128, 8, SEQ], BF16, kind="ExternalOutput").ap()
        dbg["x1"] = nc.dram_tensor("dbg_x1", [128, 16, D], F32, kind="ExternalOutput").ap()
        dbg["QK"] = nc.dram_tensor("dbg_QK", [128, 2, NCOL], BF16, kind="ExternalOutput").ap()

    stack = ExitStack()
    ARENA_B = 170 * 1024
    cols = stack.enter_context(nc.sbuf_tensor("cols_sb", [128, NCOLS], F32))
    cmat = stack.enter_context(nc.sbuf_tensor("cmat_sb", [128, 4, 128], BF16))
    bscr = stack.enter_context(nc.sbuf_tensor("bscr", [128, 8], F32))
    wsl = stack.enter_context(nc.sbuf_tensor("wsl", [128, 4, 4096], BF16))
    arena = stack.enter_context(nc.sbuf_tensor("arena", [128, ARENA_B // 2], BF16))
    ps = stack.enter_context(nc.psum_tensor("ps", [128, 8, 512], F32))

    ident = cmat[:, 0, :]
    c256 = cmat[:, 1, :]
    c128 = cmat[:, 2, :]
    c1024 = cmat[:, 3, :]

    def view(off_b, shape, dt=BF16):
        n = int(np.prod(shape))
        esz = 2 if dt == BF16 else 4
        assert off_b % 4 == 0 and off_b + n * esz <= ARENA_B, (off_b, shape)
        a = arena[:, off_b // 2: off_b // 2 + n * esz // 2]
        if dt != BF16:
            a = a.bitcast(dt)
        if len(shape) == 2:
            a = a.rearrange("p (a b) -> p a b", a=shape[0])
        elif len(shape) == 3:
            a = a.rearrange("p (a b c) -> p a b c", a=shape[0], b=shape[1])
        return a

    KB = 1024
    A_TRIG = ARENA_B - 2 * NCOL * 4
    trig = view(A_TRIG, [2, NCOL], F32)
    cosT = trig[:, 0, :]
    sinT = trig[:, 1, :]
    bctr = [0]

    def barrier():
        i = bctr[0] % 8
        bctr[0] += 1
        P.barrier("vector", lambda e: e.memset(bscr[:, i:i + 1], 0.0))

    def pst_view(bank):
        return ps[:, bank, :].bitcast(BF16).rearrange("p (k t) -> p k t", k=8)

    def dma_sync(out, in_, r=(), w=(), key=None):
        return P.op("sync", lambda e: e.dma_start(out=out, in_=in_), r=r, w=w, dma=key)

    def dma_cast(out, in_, r=(), w=(), key=None):
        return P.op("gpsimd", lambda e: e.dma_start(out=out, in_=in_), r=r, w=w, dma=key)

    def mm(out, lhsT, rhs, bank, start, stop, r=()):
        return P.op("tensor", lambda e: e.matmul(out, lhsT=lhsT, rhs=rhs, start=start, stop=stop),
                    r=r, w=[("ps", bank)])

    def act(out, in_, func, r=(), w=(), **kw):
        return P.op("scalar", lambda e: e.activation(out=out, in_=in_, func=func, **kw), r=r, w=w)

    def dve(fn, r=(), w=()):
        return P.op("vector", fn, r=r, w=w)

    wctr = [0]

    def wslot():
        s = wctr[0] % 4
        wctr[0] += 1
        return s

    def load_w_slab(src_ap, nk, ncol, slot=None, key=None):
        if slot is None:
            slot = wslot()
        v = wsl[:, slot, 0:nk * ncol].rearrange("p (k n) -> p k n", k=nk)
        dma_cast(v, src_ap.rearrange("(k p) n -> p k n", p=128), w=[("wsl", slot)], key=("wsl", slot))
        return slot, v

    def gated_load(wsrc, gate_col0):
        out = []
        for half in range(2):
            out.append(load_w_slab(wsrc[:, half * 512:(half + 1) * 512], 8, 512))
            out.append(load_w_slab(w_in[:, gate_col0 + half * 512: gate_col0 + (half + 1) * 512], 8, 512))
        return out

    dma_sync(cols[:], cols_d[:, :], w=["cols"], key="c0")
    dma_sync(cmat[:], cmat_d[:, :, :], w=["cmat"], key="c1")
    dma_sync(trig[:, :, :], trig_d[:, :, :], w=["trig"], key="c2")

    def norm_setup(hT, off_scratch, src_tile, gcol, key_out, x1=None):
        o = off_scratch
        if x1 is None:
            xs = view(o, [3, D], F32);  o += 12 * KB
        junk = view(o, [D], BF16);      o += 2 * KB
        xn = view(o, [2, D], BF16);     o += 4 * KB
        ss = view(o, [64], F32);        o += 256
        sd = view(o, [64], F32);        o += 256
        rs = view(o, [64], F32);        o += 256
        gb_src = cols[:, gcol:gcol + 8]

        def prep(i):
            src, rows, c0 = src_tile(i)
            sl = i % 3
            if x1 is None:
                dma_sync(xs[0:rows, sl, :], src, w=[("xs", sl)], key=("xs", sl))
                xin = xs[0:rows, sl, :]
                rkey = ("xs", sl)
            else:
                xin = x1[:, i, :]
                rkey = ("x1", i)
            act(junk[0:rows, :], xin, AF.Square, r=[rkey], w=["junk", ("ss", i)],
                accum_out=ss[0:rows, i:i + 1])
            act(sd[0:rows, i:i + 1], ss[0:rows, i:i + 1], AF.Sqrt, r=[("ss", i)], w=[("sd", i)],
                scale=1.0 / D, bias=EPS)
            dve(lambda e, o=rs[0:rows, i:i + 1], a=sd[0:rows, i:i + 1]: e.reciprocal(out=o, in_=a),
                r=[("sd", i)], w=[("rs", i)])
            dve(lambda e, o=xn[0:rows, i % 2, :], a=xin, s=rs[0:rows, i:i + 1]:
                e.tensor_scalar_mul(out=o, in0=a, scalar1=s),
                r=[rkey, ("rs", i)], w=[("xn", i % 2)])

        def tr(i):
            src, rows, c0 = src_tile(i)
            bank = 6 + (i % 2)
            pv = pst_view(bank)
            for k in range(8):
                P.op("tensor", lambda e, o=pv[:, k, 0:rows], a=xn[0:rows, i % 2, k * 128:(k + 1) * 128],
                     idn=ident[0:rows, 0:rows]: e.transpose(out=o, in_=a, identity=idn),
                     r=[("xn", i % 2), "cmat"], w=[("ps", bank)])
            gb = gb_src.unsqueeze(2).to_broadcast([128, 8, rows])
            dve(lambda e, o=hT[:, :, c0:c0 + rows], a=pv[:, :, 0:rows], g=gb:
                e.tensor_tensor(out=o, in0=a, in1=g, op=ALU.mult),
                r=["cols"], w=[("ps", bank), (key_out, i)])

        def do_tile(i):
            prep(i)
            tr(i)
        do_tile.prep = prep
        do_tile.tr = tr
        return do_tile

    def norm_to_T(hT, off_scratch, src_tile, gcol, key_out, ntiles=17, x1=None):
        do_tile = norm_setup(hT, off_scratch, src_tile, gcol, key_out, x1=x1)
        for i in range(ntiles + 1):
            if i < ntiles:
                do_tile.prep(i)
            if i >= 1:
                do_tile.tr(i - 1)

    def x_tiles(i):
        if i < 16:
            return x[128 * i:128 * i + 128, :], 128, NMETA + 128 * i
        return meta[:, :], NMETA, 0

    def hkeys(c0, n, name="hT"):
        ks = []
        if c0 < NMETA:
            ks.append((name, 16))
        lo = max(c0, NMETA) - NMETA
        hi = c0 + n - NMETA
        if hi > lo:
            for i in range(lo // 128, (hi - 1) // 128 + 1):
                ks.append((name, i))
        return ks

    A_CQ = 0
    A_CKV = 8 * KB
    A_KR = A_CKV + 4160
    A_HT = 17 * KB
    A_SCR = 50 * KB
    A_LAT = 70 * KB
    cqnT = view(A_CQ, [2, SEQ])
    ckvnT = view(A_CKV, [NCOL])
    kropeT = view(A_KR, [NCOL])
    hT = view(A_HT, [8, NCOL])

    lat_slot, wlat = load_w_slab(w_in[:, 0:OFF_GLU], 8, OFF_GLU)
    wuq = view(110 * KB, [2, 1536])
    wukv = view(116 * KB, [2048])
    dma_cast(wuq[:, :, :], w_uq.rearrange("(c p) n -> p c n", p=128), w=["wuq"], key="wuq")
    dma_cast(wukv[:, :], w_ukv[:, :], w=["wukv"], key="wukv")
    wrot = view(A_LAT, [8, 96])
    dve(lambda e: e.tensor_copy(out=wrot[:, :, 0:64], in_=wlat[:, :, 320:384]),
        r=[("wsl", lat_slot)], w=["wrot"])
    dve(lambda e: e.tensor_scalar_mul(out=wrot[:, :, 64:80], in0=wlat[:, :, 400:416], scalar1=-1.0),
        r=[("wsl", lat_slot)], w=["wrot"])
    dve(lambda e: e.tensor_copy(out=wrot[:, :, 80:96], in_=wlat[:, :, 384:400]),
        r=[("wsl", lat_slot)], w=["wrot"])

    norm_to_T(hT, A_SCR, x_tiles, C_PRE, "hT")

    if DEBUG:
        dma_sync(dbg["hT"][:, :, :], hT[:, :, :], r=[("hT", i) for i in range(17)], key="dbg")

    sq = view(A_LAT + 2 * KB, [2, 512])
    sdt = view(A_LAT + 4 * KB, [512], F32)
    rst = view(A_LAT + 6 * KB, [512], F32)
    t1 = view(A_LAT + 8 * KB, [512], F32)
    t2 = view(A_LAT + 10 * KB, [512], F32)

    def rope_evac(out_ap, bank1, bank2, n, tc0, tag, t1, t2):
        dve(lambda e: e.tensor_tensor(out=t1[64:96, 0:n], in0=ps[64:96, bank1, 0:n],
                                      in1=cosT[64:96, tc0:tc0 + n], op=ALU.mult),
            r=["trig"], w=[("ps", bank1), "t1"])
        dve(lambda e: e.tensor_tensor(out=t2[64:96, 0:n], in0=ps[64:96, bank2, 0:n],
                                      in1=sinT[64:96, tc0:tc0 + n], op=ALU.mult),
            r=["trig"], w=[("ps", bank2), "t2"])
        dve(lambda e: e.tensor_tensor(out=out_ap, in0=t1[64:96, 0:n], in1=t2[64:96, 0:n], op=ALU.add),
            r=["t1", "t2"], w=[tag])

    groups = [(NMETA + 512 * g, 512, g) for g in range(4)] + [(0, NMETA, 4)]
    for (c0, n, g) in groups:
        hk = hkeys(c0, n)
        real = g < 4
        targets = []
        if real:
            targets += [(0, 128, 0, 128), (1, 128, 128, 256)]
        targets += [(2, 128, 256, 384), (3, 96, 320, 416)]
        for (bank, M, a, b) in targets:
            for k in range(8):
                mm(ps[0:M, bank, 0:n], wlat[:, k, a:b], hT[:, k, c0:c0 + n], bank, k == 0, k == 7,
                   r=hk + [("wsl", lat_slot)])
        for k in range(8):
            mm(ps[0:96, 4, 0:n], wrot[:, k, :], hT[:, k, c0:c0 + n], 4, k == 0, k == 7, r=hk + ["wrot"])
        if real:
            tcol = c0 - NMETA
            for j in range(2):
                act(sq[:, j, 0:n], ps[:, j, 0:n], AF.Square, w=[("ps", j), ("sq", j)])
            mm(ps[:, 5, 0:n], c256, sq[:, 0, 0:n], 5, True, False, r=[("sq", 0), "cmat"])
            mm(ps[:, 5, 0:n], c256, sq[:, 1, 0:n], 5, False, True, r=[("sq", 1), "cmat"])
            act(sdt[:, 0:n], ps[:, 5, 0:n], AF.Sqrt, w=[("ps", 5), "sdt"], bias=EPS, scale=1.0)
            dve(lambda e, n=n: e.reciprocal(out=rst[:, 0:n], in_=sdt[:, 0:n]), r=["sdt"], w=["rst"])
            for j in range(2):
                dve(lambda e, j=j, n=n, tcol=tcol: e.scalar_tensor_tensor(
                    out=cqnT[:, j, tcol:tcol + n], in0=ps[:, j, 0:n], scalar=cols[:, C_QG + j:C_QG + j + 1],
                    in1=rst[:, 0:n], op0=ALU.mult, op1=ALU.mult),
                    r=["rst", "cols"], w=[("ps", j), ("cqnT", g)])
        act(sq[:, 0, 0:n], ps[:, 2, 0:n], AF.Square, w=[("ps", 2), ("sq", 0)])
        mm(ps[:, 5, 0:n], c128, sq[:, 0, 0:n], 5, True, True, r=[("sq", 0), "cmat"])
        act(sdt[:, 0:n], ps[:, 5, 0:n], AF.Sqrt, w=[("ps", 5), "sdt"], bias=EPS, scale=1.0)
        dve(lambda e, n=n: e.reciprocal(out=rst[:, 0:n], in_=sdt[:, 0:n]), r=["sdt"], w=["rst"])
        dve(lambda e, n=n, c0=c0: e.scalar_tensor_tensor(
            out=ckvnT[:, c0:c0 + n], in0=ps[:, 2, 0:n], scalar=cols[:, C_KVG:C_KVG + 1],
            in1=rst[:, 0:n], op0=ALU.mult, op1=ALU.mult),
            r=["rst", "cols"], w=[("ps", 2), ("ckvnT", g)])
        rope_evac(kropeT[64:96, c0:c0 + n], 3, 4, n, c0, ("kropeT", g), t1, t2)

    if DEBUG:
        dma_sync(dbg["lat"][:, 0:2, NMETA:NCOL], cqnT[:, :, :], r=[("cqnT", g) for g in range(4)], key="dbg")
        dma_sync(dbg["lat"][:, 2, :], ckvnT[:, :], r=[("ckvnT", g) for g in range(5)], key="dbg")
        dma_sync(dbg["lat"][64:96, 3, :], kropeT[64:96, :], r=[("kropeT", g) for g in range(5)], key="dbg")

    barrier()

    preC = gated_load(w_ao, OFF_GATE)
    B0 = 50 * KB
    wq2 = view(B0, [2, 1536]);            B0 += 6 * KB
    Vaug = view(B0, [2, 17, 128]);        B0 += 8704
    QT = view(B0, [2, SEQ]);              B0 += 8 * KB
    KT = view(B0, [2, NCOL]);             B0 += 8256
    PT = view(B0, [4, 2, 512]);           B0 += 8 * KB
    PTm = view(B0, [2, 512]);             B0 += 2 * KB
    rc = view(B0, [2, 512], F32);         B0 += 4 * KB
    tb1 = view(B0, [512], F32);           B0 += 2 * KB
    tb2 = view(B0, [512], F32);           B0 += 2 * KB
    assert B0 <= 110 * KB, B0
    A_OT = 121 * KB
    OT = view(A_OT, [8, SEQ])

    dve(lambda e: e.tensor_copy(out=wq2[:, :, :], in_=wuq[:, :, :]), r=["wuq"], w=["wq2"])
    wuq4 = wuq.rearrange("p c (h d) -> p c h d", h=NH)
    wq24 = wq2.rearrange("p c (h d) -> p c h d", h=NH)
    for c in range(2):
        dve(lambda e, c=c: e.tensor_scalar_mul(out=wq24[:, c, :, 64:80], in0=wuq4[:, c, :, 80:96], scalar1=-1.0),
            r=["wuq"], w=["wq2"])
        dve(lambda e, c=c: e.tensor_copy(out=wq24[:, c, :, 80:96], in_=wuq4[:, c, :, 64:80]),
            r=["wuq"], w=["wq2"])
    dve(lambda e: e.memset(Vaug[:, 0, :, 64:128], 1.0), w=[("Vaug", 0)])
    dve(lambda e: e.memset(Vaug[:, 1, :, 0:64], 1.0), w=[("Vaug", 1)])
    for b in range(2):
        dve(lambda e, b=b: e.tensor_copy(out=KT[64:96, b, :], in_=kropeT[64:96, :]), w=[("KTr", b)])

    kgroups = [(NMETA + 512 * g, 512) for g in range(4)] + [(0, NMETA)]

    def proj_chunks(h):
        b = h % 2
        voff = 0 if b == 0 else 64
        chunks = []

        wvh = wukv[:, h * 128 + 64:h * 128 + 128]
        for vi, (j_lo, j_hi) in enumerate([(0, 8), (8, 16), (16, 17)]):
            def c_v(vi=vi, j_lo=j_lo, j_hi=j_hi):
                bank = 6 + (vi % 2)
                nt = j_hi - j_lo
                rows = 128 if j_lo < 16 else NMETA
                for j in range(j_lo, j_hi):
                    kc0 = NMETA + 128 * j if j < 16 else 0
                    mm(ps[0:rows, bank, (j - j_lo) * 64:(j - j_lo + 1) * 64], ckvnT[:, kc0:kc0 + rows], wvh, bank,
                       True, True, r=["wukv"])
                src = ps[0:rows, bank, 0:nt * 64].rearrange("p (t d) -> p t d", t=nt)
                dve(lambda e: e.tensor_copy(out=Vaug[0:rows, b, j_lo:j_hi, voff:voff + 64], in_=src),
                    w=[("ps", bank), ("Vaug", b)])
            chunks.append(c_v)
        for gi, (c0, n) in enumerate(kgroups):
            def c_k(gi=gi, c0=c0, n=n):
                bank = 6 + (gi % 2)
                mm(ps[0:64, bank, 0:n], wukv[:, h * 128:h * 128 + 64], ckvnT[:, c0:c0 + n], bank, True, True,
                   r=["wukv"])
                dve(lambda e: e.tensor_copy(out=KT[0:64, b, c0:c0 + n], in_=ps[0:64, bank, 0:n]),
                    w=[("ps", bank), ("KT", b)])
            chunks.append(c_k)
        for g in range(4):
            def c_q(g=g):
                for c in range(2):
                    mm(ps[0:96, 6, :], wuq[:, c, h * 96:(h + 1) * 96], cqnT[:, c, g * 512:(g + 1) * 512], 6,
                       c == 0, c == 1, r=["wuq"])
                for c in range(2):
                    mm(ps[0:96, 7, :], wq2[:, c, h * 96:(h + 1) * 96], cqnT[:, c, g * 512:(g + 1) * 512], 7,
                       c == 0, c == 1, r=["wq2"])
                dve(lambda e: e.tensor_copy(out=QT[0:64, b, g * 512:(g + 1) * 512], in_=ps[0:64, 6, :]),
                    w=[("ps", 6), ("QT", b, g)])
                rope_evac(QT[64:96, b, g * 512:(g + 1) * 512], 6, 7, 512, NMETA + g * 512, ("QTr", b, g), tb1, tb2)
            chunks.append(c_q)
        return chunks

    pair_ctr = [0]
    o_ctr = [0]
    gstate = {}

    def stage1(t):
        h, g = t["h"], t["g"]
        b = h % 2
        qk = [("QT", b, g), ("QTr", b, g)]
        kk = [("KT", b), ("KTr", b)]
        q0 = g * 512
        pc = pair_ctr[0]
        pair_ctr[0] += 1
        sb = 2 * (pc % 2)
        t["sb"] = sb
        if t["kind"] == "meta":
            t["msl"] = pc % 2
            ob = 4 + (o_ctr[0] % 2)
            gstate[(h, g)] = (ob, o_ctr[0] % 2)
            o_ctr[0] += 1
            mm(ps[0:16, sb, :], KT[0:96, b, 0:NMETA], QT[0:96, b, q0:q0 + 512], sb, True, True, r=qk + kk)
        else:
            t["slot"] = pc % 4
            los = []
            for a in range(2):
                j = t["j0"] + a
                lo = max(0, j - 4 * g) * 128
                los.append(lo)
                kc = NMETA + 128 * j
                mm(ps[:, sb + a, lo:512], KT[0:96, b, kc:kc + 128], QT[0:96, b, q0 + lo:q0 + 512], sb + a,
                   True, True, r=qk + kk)
            t["los"] = los

    def stage23(t):
        h, g = t["h"], t["g"]
        b = h % 2
        q0 = g * 512
        sb = t["sb"]
        ob, rsl = gstate[(h, g)]
        nkt = 4 * g + 4
        if t["kind"] == "meta":
            msl = t["msl"]
            act(PTm[0:16, msl, :], ps[0:16, sb, :], AF.Exp, w=[("ps", sb), ("PTm", msl)], scale=SCALE)
            mm(ps[:, ob, :], Vaug[0:16, b, 16, :], PTm[0:16, msl, :], ob, True, False,
               r=[("PTm", msl), ("Vaug", b)])
            return
        slot, los, j0 = t["slot"], t["los"], t["j0"]
        if los[0] == 0 and los[1] == 0:
            act(PT[:, slot, :, :], ps[:, sb:sb + 2, :], AF.Exp,
                w=[("ps", sb), ("ps", sb + 1), ("PT", slot, 0), ("PT", slot, 1)], scale=SCALE)
        else:
            for a in range(2):
                lo = los[a]
                act(PT[:, slot, a, lo:512], ps[:, sb + a, lo:512], AF.Exp,
                    w=[("ps", sb + a), ("PT", slot, a)], scale=SCALE)
        for a in range(2):
            j = j0 + a
            lo = los[a]
            if j >= 4 * g:
                dve(lambda e, a=a, lo=lo: e.memset(PT[64:128, slot, a, lo:lo + 64], 0.0), w=[("PT", slot, a)])
            mm(ps[:, ob, lo:512], Vaug[:, b, j, :], PT[:, slot, a, lo:512], ob, False, j == nkt - 1,
               r=[("PT", slot, a), ("Vaug", b)])
        if j0 + 2 == nkt:
            if b == 0:
                dlo, slo = 0, 64
            else:
                dlo, slo = 64, 0
            act(rc[dlo:dlo + 64, rsl, :], ps[slo:slo + 64, ob, :], AF.Ln, w=[("ps", ob), ("rc", rsl)])
            act(rc[dlo:dlo + 64, rsl, :], rc[dlo:dlo + 64, rsl, :], AF.Exp, r=[("rc", rsl)], w=[("rc", rsl)],
                scale=-1.0)
            dve(lambda e: e.tensor_tensor(out=OT[dlo:dlo + 64, h // 2, q0:q0 + 512], in0=ps[dlo:dlo + 64, ob, :],
                                          in1=rc[dlo:dlo + 64, rsl, :], op=ALU.mult),
                r=[("rc", rsl)], w=[("ps", ob), ("OT", h // 2, g)])

    for ch in proj_chunks(0):
        ch()
    if DEBUG:
        dma_sync(dbg["QK"][0:96, 0, NMETA:NCOL], QT[0:96, 0, :], r=[("QT", 0, g) for g in range(4)] +
                 [("QTr", 0, g) for g in range(4)], key="dbg")
        dma_sync(dbg["QK"][0:96, 1, :], KT[0:96, 0, :], r=[("KT", 0), ("KTr", 0)], key="dbg")
    tasks = []
    for h in range(NH):
        pend = proj_chunks(h + 1) if h + 1 < NH else []
        ht = []
        for g in range(4):
            ht.append(dict(kind="meta", h=h, g=g))
            for j0 in range(0, 4 * g + 4, 2):
                ht.append(dict(kind="pair", h=h, g=g, j0=j0))
        for ti, t in enumerate(ht):
            t["extra"] = []
            if ti >= 2 and ti % 2 == 0 and pend:
                t["extra"].append(pend.pop(0))
        while pend:
            ht[-1]["extra"].append(pend.pop(0))
        tasks += ht
    for i in range(len(tasks) + 1):
        if i < len(tasks):
            stage1(tasks[i])
        if i >= 1:
            stage23(tasks[i - 1])
            for ch in tasks[i - 1]["extra"]:
                ch()

    if DEBUG:
        dma_sync(dbg["OT"][:, :, :], OT[:, :, :], r=[("OT", k, g) for k in range(8) for g in range(4)], key="dbg")

    barrier()

    maT = view(50 * KB, [8, SEQ])
    sg = view(0, [2, 512])
    tmpb = view(2 * KB, [2, 512])

    def gated_proj(pre, src_rhs, src_keys, accumulate, after_half0=None):
        cnt = 0
        for half in range(2):
            if half == 1 and after_half0 is not None:
                after_half0()
            s_w, wv = pre[2 * half]
            s_g, gv = pre[2 * half + 1]
            for mi in range(4):
                m = half * 4 + mi
                for g in range(4):
                    by = 2 * (cnt % 2)
                    bg = by + 1
                    cnt += 1
                    for k in range(8):
                        mm(ps[:, by, :], wv[:, k, mi * 128:(mi + 1) * 128], src_rhs(k, g), by, k == 0, k == 7,
                           r=[("wsl", s_w)] + src_keys(k, g))
                    for k in range(8):
                        mm(ps[:, bg, :], gv[:, k, mi * 128:(mi + 1) * 128],
                           hT[:, k, NMETA + g * 512:NMETA + (g + 1) * 512], bg, k == 0, k == 7,
                           r=[("wsl", s_g)] + hkeys(NMETA + g * 512, 512))
                    sl = cnt % 2
                    act(sg[:, sl, :], ps[:, bg, :], AF.Tanh, w=[("ps", bg), ("sg", sl)], scale=0.5)
                    if not accumulate:
                        dve(lambda e, m=m, g=g, by=by, sl=sl: e.scalar_tensor_tensor(
                            out=maT[:, m, g * 512:(g + 1) * 512], in0=sg[:, sl, :], scalar=1.0, in1=ps[:, by, :],
                            op0=ALU.add, op1=ALU.mult),
                            r=[("sg", sl)], w=[("ps", by), ("maT", m, g)])
                    else:
                        dve(lambda e, by=by, sl=sl: e.scalar_tensor_tensor(
                            out=tmpb[:, sl, :], in0=sg[:, sl, :], scalar=1.0, in1=ps[:, by, :],
                            op0=ALU.add, op1=ALU.mult),
                            r=[("sg", sl)], w=[("ps", by), ("tmpb", sl)])
                        dve(lambda e, m=m, g=g, sl=sl: e.tensor_tensor(
                            out=maT[:, m, g * 512:(g + 1) * 512], in0=maT[:, m, g * 512:(g + 1) * 512],
                            in1=tmpb[:, sl, :], op=ALU.add),
                            r=[("tmpb", sl)], w=[("maT", m, g)])

    glu_slabs = {}

    def load_glu(half):
        glu_slabs[("a", half)] = load_w_slab(w_in[:, OFF_GLU + half * 512:OFF_GLU + (half + 1) * 512], 8, 512)
        glu_slabs[("g", half)] = load_w_slab(w_in[:, OFF_GLU + D + half * 512:OFF_GLU + D + (half + 1) * 512], 8, 512)

    gated_proj(preC, lambda k, g: OT[:, k, g * 512:(g + 1) * 512], lambda k, g: [("OT", k, g)],
               False, after_half0=lambda: load_glu(0))
    load_glu(1)

    barrier()

    UP = 30 + NCOL
    upad = view(82 * KB, [8, UP + 2])
    D1 = 82 * KB + 33536
    diag = view(D1, [2, CONV_K, 128]);   D1 += 2 * CONV_K * 256
    tn = view(D1, [2, 512], F32);        D1 += 4 * KB
    sgd = view(D1, [2, 512]);            D1 += 2 * KB
    assert D1 <= 138 * KB, D1
    D2 = 4 * KB
    ysq = view(D2, [2, 512]);            D2 += 2 * KB
    mean_sb = view(D2, [512], F32);      D2 += 2 * KB
    var_sb = view(D2, [512], F32);       D2 += 2 * KB
    rstd_sb = view(D2, [512], F32);      D2 += 2 * KB
    nmr_sb = view(D2, [512], F32);       D2 += 2 * KB
    assert D2 <= 17 * KB
    yc = view(138 * KB, [8, SEQ])

    for m in range(8):
        dve(lambda e, m=m: e.memset(upad[:, m, 0:30], 0.0), w=[("upad", m, 5)])
    cnt = 0
    for m in range(8):
        half, mi = m // 4, m % 4
        s_a, av = glu_slabs[("a", half)]
        s_g, gv = glu_slabs[("g", half)]
        for (c0, n, g) in groups:
            ba = 2 * (cnt % 2)
            bg = ba + 1
            sl = cnt % 2
            cnt += 1
            hk = hkeys(c0, n)
            for k in range(8):
                mm(ps[:, ba, 0:n], av[:, k, mi * 128:(mi + 1) * 128], hT[:, k, c0:c0 + n], ba, k == 0, k == 7,
                   r=hk + [("wsl", s_a)])
            for k in range(8):
                mm(ps[:, bg, 0:n], gv[:, k, mi * 128:(mi + 1) * 128], hT[:, k, c0:c0 + n], bg, k == 0, k == 7,
                   r=hk + [("wsl", s_g)])
            act(sgd[:, sl, 0:n], ps[:, bg, 0:n], AF.Sigmoid, w=[("ps", bg), ("sgd", sl)])
            dve(lambda e, m=m, c0=c0, n=n, ba=ba, sl=sl: e.tensor_tensor(
                out=upad[:, m, 30 + c0:30 + c0 + n], in0=ps[:, ba, 0:n], in1=sgd[:, sl, 0:n], op=ALU.mult),
                r=[("sgd", sl)], w=[("ps", ba), ("upad", m, g)])
    preD = gated_load(w_co, OFF_GATE + D)
    cnt = 0
    for m in range(8):
        db = m % 2
        for k in range(CONV_K):
            dve(lambda e, m=m, k=k, db=db: e.tensor_scalar_mul(
                out=diag[:, db, k, :], in0=ident, scalar1=cols[:, C_CW + m * CONV_K + k:C_CW + m * CONV_K + k + 1]),
                r=["cols", "cmat"], w=[("diag", db)])
        for g in range(4):
            bank = 4 + (cnt % 2)
            cnt += 1
            p0 = NMETA + g * 512
            for k in range(CONV_K):
                mm(ps[:, bank, :], diag[:, db, k, :], upad[:, m, p0 + k:p0 + k + 512], bank, k == 0, k == CONV_K - 1,
                   r=[("diag", db)] + [("upad", m, gg) for gg in range(6)])
            act(yc[:, m, g * 512:(g + 1) * 512], ps[:, bank, :], AF.Identity, r=["cols"],
                w=[("ps", bank), ("yc", m, g)], bias=cols[:, C_CB + m:C_CB + m + 1], scale=1.0)
    def ln_stats(g):
        gs = slice(g * 512, (g + 1) * 512)
        for m in range(8):
            sl = m % 2
            dve(lambda e, m=m, sl=sl: e.tensor_tensor(out=ysq[:, sl, :], in0=yc[:, m, gs], in1=yc[:, m, gs],
                                                      op=ALU.mult),
                r=[("yc", m, g)], w=[("ysq", sl)])
            mm(ps[:, 6, :], c1024, yc[:, m, gs], 6, m == 0, m == 7, r=[("yc", m, g), "cmat"])
            mm(ps[:, 7, :], c1024, ysq[:, sl, :], 7, m == 0, m == 7, r=[("ysq", sl), "cmat"])
        dve(lambda e: e.tensor_copy(out=mean_sb[:, :], in_=ps[:, 6, :]), w=[("ps", 6), "mean"])
        dve(lambda e: e.tensor_tensor(out=var_sb[:, :], in0=mean_sb[:, :], in1=mean_sb[:, :], op=ALU.mult),
            r=["mean"], w=["var"])
        dve(lambda e: e.tensor_tensor(out=var_sb[:, :], in0=ps[:, 7, :], in1=var_sb[:, :], op=ALU.subtract),
            r=["var"], w=[("ps", 7), "var"])
        act(var_sb[:, :], var_sb[:, :], AF.Sqrt, r=["var"], w=["var"], bias=EPS, scale=1.0)
        dve(lambda e: e.reciprocal(out=rstd_sb[:, :], in_=var_sb[:, :]), r=["var"], w=["rstd"])
        dve(lambda e: e.tensor_tensor(out=nmr_sb[:, :], in0=mean_sb[:, :], in1=rstd_sb[:, :], op=ALU.mult),
            r=["mean", "rstd"], w=["nmr"])

    def ln_apply(g, m):
        gs = slice(g * 512, (g + 1) * 512)
        sl = m % 2
        dve(lambda e: e.tensor_tensor(out=tn[:, sl, :], in0=yc[:, m, gs], in1=rstd_sb[:, :], op=ALU.mult),
            r=[("yc", m, g), "rstd"], w=[("tn", sl)])
        dve(lambda e: e.tensor_tensor(out=tn[:, sl, :], in0=tn[:, sl, :], in1=nmr_sb[:, :], op=ALU.subtract),
            r=["nmr"], w=[("tn", sl)])
        act(yc[:, m, gs], tn[:, sl, :], AF.Silu, r=[("tn", sl), "cols"], w=[("yc", m, g)],
            scale=cols[:, C_LNG + m:C_LNG + m + 1], bias=cols[:, C_LNB + m:C_LNB + m + 1])

    gcnt = [0]

    def gated_unit(g, m):
        half, mi = m // 4, m % 4
        s_w, wv = preD[2 * half]
        s_g, gv = preD[2 * half + 1]
        by = 2 * (gcnt[0] % 2)
        bg = by + 1
        gcnt[0] += 1
        sl = gcnt[0] % 2
        for k in range(8):
            mm(ps[:, by, :], wv[:, k, mi * 128:(mi + 1) * 128], yc[:, k, g * 512:(g + 1) * 512], by, k == 0, k == 7,
               r=[("wsl", s_w), ("yc", k, g)])
        for k in range(8):
            mm(ps[:, bg, :], gv[:, k, mi * 128:(mi + 1) * 128],
               hT[:, k, NMETA + g * 512:NMETA + (g + 1) * 512], bg, k == 0, k == 7,
               r=[("wsl", s_g)] + hkeys(NMETA + g * 512, 512))
        act(sg[:, sl, :], ps[:, bg, :], AF.Tanh, w=[("ps", bg), ("sg", sl)], scale=0.5)
        dve(lambda e: e.scalar_tensor_tensor(out=tmpb[:, sl, :], in0=sg[:, sl, :], scalar=1.0, in1=ps[:, by, :],
                                             op0=ALU.add, op1=ALU.mult),
            r=[("sg", sl)], w=[("ps", by), ("tmpb", sl)])
        dve(lambda e: e.tensor_tensor(out=maT[:, m, g * 512:(g + 1) * 512], in0=maT[:, m, g * 512:(g + 1) * 512],
                                      in1=tmpb[:, sl, :], op=ALU.add),
            r=[("tmpb", sl)], w=[("maT", m, g)])

    ln_stats(0)
    for m in range(8):
        ln_apply(0, m)
    for g in range(4):
        if g < 3:
            ln_stats(g + 1)
        for m in range(8):
            gated_unit(g, m)
            if g < 3:
                ln_apply(g + 1, m)
    wo_slabs = [load_w_slab(w_out[:, nh * 512:(nh + 1) * 512], 8, 512) for nh in range(2)]

    if DEBUG:
        dma_sync(dbg["maT"][:, :, :], maT[:, :, :], r=[("maT", m, g) for m in range(8) for g in range(4)], key="dbg")

    barrier()

    h2T = view(0, [8, SEQ])
    gpost = view(32 * KB, [2, D], F32)
    xs2 = view(40 * KB, [2, D], F32)
    E_SCR = 82 * KB
    junk = view(E_SCR, [D])
    ssz = view(E_SCR + 2 * KB, [64], F32)
    sdz = view(E_SCR + 2 * KB + 256, [64], F32)
    rz = view(E_SCR + 2 * KB + 512, [64], F32)
    tmpf = view(E_SCR + 3 * KB, [2, 512], F32)
    x1 = view(106 * KB, [16, D], F32)
    ffn_norm_tile = norm_setup(h2T, E_SCR + 7 * KB, lambda i: (None, 128, 128 * i), C_FPRE, "h2T", x1=x1)

    def ffn_pair_load(cp):
        slot = wslot()
        gv = wsl[:, slot, 0:2048].rearrange("p (k n) -> p k n", k=8)
        uv = wsl[:, slot, 2048:4096].rearrange("p (k n) -> p k n", k=8)
        dma_cast(gv, w_fi[:, cp * 256:(cp + 1) * 256].rearrange("(k p) n -> p k n", p=128),
                 w=[("wslg", slot)], key=("wslg", slot))
        dma_cast(uv, w_fi[:, DFF + cp * 256:DFF + (cp + 1) * 256].rearrange("(k p) n -> p k n", p=128),
                 w=[("wslu", slot)], key=("wslu", slot))
        return slot, gv, uv

    pre_pairs = [ffn_pair_load(0), ffn_pair_load(1)]
    dma_sync(gpost[:, :, :], gpost_d[:, :, :], w=["gpost"], key="c3")
    def e_mm(i):
        dma_sync(xs2[:, i % 2, :], x[128 * i:128 * i + 128, :], w=[("xs2", i % 2)], key=("xs2", i % 2))
        for nh in range(2):
            s_w, wv = wo_slabs[nh]
            bank = 2 * (i % 3) + nh
            for k in range(8):
                mm(ps[:, bank, :], maT[:, k, i * 128:(i + 1) * 128], wv[:, k, :], bank, k == 0, k == 7,
                   r=[("wsl", s_w)] + [("maT", k, i // 4)])

    def e_evac(i):
        sl = i % 2
        for nh in range(2):
            bank = 2 * (i % 3) + nh
            act(junk[:, 0:512], ps[:, bank, :], AF.Square, w=[("ps", bank), "junk", ("ssz", i, nh)],
                accum_out=ssz[:, 2 * i + nh:2 * i + nh + 1], scale=0.5)
        dve(lambda e: e.tensor_tensor(out=sdz[:, i:i + 1], in0=ssz[:, 2 * i:2 * i + 1],
                                      in1=ssz[:, 2 * i + 1:2 * i + 2], op=ALU.add),
            r=[("ssz", i, 0), ("ssz", i, 1)], w=[("sdz", i)])
        act(sdz[:, i:i + 1], sdz[:, i:i + 1], AF.Sqrt, r=[("sdz", i)], w=[("sdz", i)], scale=1.0 / D, bias=EPS)
        dve(lambda e: e.reciprocal(out=rz[:, i:i + 1], in_=sdz[:, i:i + 1]), r=[("sdz", i)], w=[("rz", i)])
        dve(lambda e: e.tensor_scalar_mul(out=rz[:, i:i + 1], in0=rz[:, i:i + 1], scalar1=0.5),
            r=[("rz", i)], w=[("rz", i)])
        for nh in range(2):
            bank = 2 * (i % 3) + nh
            dve(lambda e, nh=nh, bank=bank: e.scalar_tensor_tensor(
                out=tmpf[:, nh, :], in0=ps[:, bank, :], scalar=rz[:, i:i + 1],
                in1=gpost[:, 0, nh * 512:(nh + 1) * 512], op0=ALU.mult, op1=ALU.mult),
                r=[("rz", i), "gpost"], w=[("ps", bank), ("tmpf", nh)])
            dve(lambda e, nh=nh: e.tensor_tensor(
                out=x1[:, i, nh * 512:(nh + 1) * 512], in0=tmpf[:, nh, :], in1=xs2[:, sl, nh * 512:(nh + 1) * 512],
                op=ALU.add),
                r=[("tmpf", nh), ("xs2", sl)], w=[("x1", i)])

    for it in range(16 + 2):
        if it < 16:
            e_mm(it)
        if 1 <= it <= 16:
            e_evac(it - 1)
            ffn_norm_tile.prep(it - 1)
        if 2 <= it:
            ffn_norm_tile.tr(it - 2)
    if DEBUG:
        dma_sync(dbg["x1"][:, :, :], x1[:, :, :], r=[("x1", i) for i in range(16)], key="dbg")

    barrier()

    F0 = 32 * KB
    actT = view(F0, [NFC, 512]);                 F0 += NFC * 1024
    wfo = view(F0, [2, 11, 512]);                F0 += 2 * 11 * 1024
    slu = view(F0, [2, 512], F32);               F0 += 4 * KB
    fb0 = view(F0, [4, 512], F32);               F0 += 8 * KB
    tmpo = view(F0, [2, 512], F32);              F0 += 4 * KB
    junkf = view(F0, [512]);                     F0 += 1 * KB
    ssf = view(F0, [64], F32);                   F0 += 256
    sdf = view(F0, [64], F32);                   F0 += 256
    rf = view(F0, [64], F32);                    F0 += 256
    assert F0 <= 98 * KB, F0
    gpost_f = view(98 * KB, [2, D], F32)
    dma_sync(gpost_f[:, :, :], gpost_d[:, :, :], w=["gpostf"], key="c4")

    wfo_ctr = [0]
    for q in range(4):
        qs = slice(q * 512, (q + 1) * 512)
        hk = [("h2T", i) for i in range(4 * q, 4 * q + 4)]
        cnt = 0
        for cp in range(NFC // 2):
            if q == 0 and cp < 2:
                slot, gv, uv = pre_pairs[cp]
            else:
                slot, gv, uv = ffn_pair_load(cp)
            for ci in range(2):
                c = cp * 2 + ci
                bg = 4 + 2 * (cnt % 2)
                bu = bg + 1
                sl = cnt % 2
                cnt += 1
                for k in range(8):
                    mm(ps[:, bg, :], gv[:, k, ci * 128:(ci + 1) * 128], h2T[:, k, qs], bg, k == 0, k == 7,
                       r=hk + [("wslg", slot)])
                for k in range(8):
                    mm(ps[:, bu, :], uv[:, k, ci * 128:(ci + 1) * 128], h2T[:, k, qs], bu, k == 0, k == 7,
                       r=hk + [("wslu", slot)])
                act(slu[:, sl, :], ps[:, bg, :], AF.Silu, w=[("ps", bg), ("slu", sl)])
                dve(lambda e, c=c, bu=bu, sl=sl: e.tensor_tensor(out=actT[:, c, :], in0=ps[:, bu, :],
                                                                 in1=slu[:, sl, :], op=ALU.mult),
                    r=[("slu", sl)], w=[("ps", bu), ("actT", c)])
        for nh in range(2):
            for ch in range(2):
                wb = wfo_ctr[0] % 2
                wfo_ctr[0] += 1
                dma_cast(wfo[:, wb, :, :],
                         w_fo[ch * 1408:(ch + 1) * 1408, nh * 512:(nh + 1) * 512].rearrange("(k p) n -> p k n", p=128),
                         w=[("wfo", wb)], key=("wfo", wb))
                for it in range(4):
                    for cc in range(11):
                        c = ch * 11 + cc
                        mm(ps[:, it, :], actT[:, c, it * 128:(it + 1) * 128], wfo[:, wb, cc, :], it,
                           c == 0, c == NFC - 1, r=[("actT", c), ("wfo", wb)])
            for it in range(4):
                i = 4 * q + it
                act(junkf[:, :], ps[:, it, :], AF.Square, w=[("ps", it), "junkf", ("ssf", i, nh)],
                    accum_out=ssf[:, 2 * i + nh:2 * i + nh + 1])
                if nh == 0:
                    dve(lambda e, it=it: e.tensor_copy(out=fb0[:, it, :], in_=ps[:, it, :]),
                        w=[("ps", it), ("fb0", it)])
                else:
                    dve(lambda e, i=i: e.tensor_tensor(out=sdf[:, i:i + 1], in0=ssf[:, 2 * i:2 * i + 1],
                                                       in1=ssf[:, 2 * i + 1:2 * i + 2], op=ALU.add),
                        r=[("ssf", i, 0), ("ssf", i, 1)], w=[("sdf", i)])
                    act(sdf[:, i:i + 1], sdf[:, i:i + 1], AF.Sqrt, r=[("sdf", i)], w=[("sdf", i)],
                        scale=1.0 / D, bias=EPS)
                    dve(lambda e, i=i: e.reciprocal(out=rf[:, i:i + 1], in_=sdf[:, i:i + 1]),
                        r=[("sdf", i)], w=[("rf", i)])
                    for hh in range(2):
                        src = fb0[:, it, :] if hh == 0 else ps[:, it, :]
                        wk = [("tmpo", hh)] + ([("ps", it)] if hh == 1 else [])
                        rk = [("rf", i), "gpostf"] + ([("fb0", it)] if hh == 0 else [])
                        dve(lambda e, i=i, hh=hh, src=src: e.scalar_tensor_tensor(
                            out=tmpo[:, hh, :], in0=src, scalar=rf[:, i:i + 1],
                            in1=gpost_f[:, 1, hh * 512:(hh + 1) * 512], op0=ALU.mult, op1=ALU.mult),
                            r=rk, w=wk)
                        dve(lambda e, i=i, hh=hh: e.tensor_tensor(
                            out=x1[:, i, hh * 512:(hh + 1) * 512], in0=x1[:, i, hh * 512:(hh + 1) * 512],
                            in1=tmpo[:, hh, :], op=ALU.add),
                            r=[("tmpo", hh)], w=[("x1o", i)])
                    dma_sync(y[128 * i:128 * i + 128, :], x1[:, i, :], r=[("x1o", i)], w=[("yout", i)],
                             key=("yout", i % 4))

    if DEBUG:
        P.barrier(eng="sync")
    else:
        P.op("sync", None, r=[("yout", i) for i in range(16)])

    ecnt, nd = P.emit(nc, stack)
    stack.close()
    return nc


def _host_consts():
    bf = ml_dtypes.bfloat16
    cmat = np.zeros((128, 4, 128), np.float32)
    cmat[:, 0, :] = np.eye(128, dtype=np.float32)
    cmat[:, 1, :] = 1.0 / 256.0
    cmat[:, 2, :] = 1.0 / 128.0
    cmat[:, 3, :] = 1.0 / 1024.0
    pos = np.arange(NCOL, dtype=np.float32)
    inv = (np.float32(10000.0) ** (-np.arange(0, 32, 2, dtype=np.float32) / np.float32(32))).astype(np.float32)
    ang = pos[:, None] * inv[None, :]
    cos = np.cos(ang).astype(np.float32).T
    sin = np.sin(ang).astype(np.float32).T
    trig = np.zeros((128, 2, NCOL), np.float32)
    trig[64:80, 0] = cos
    trig[80:96, 0] = cos
    trig[64:80, 1] = sin
    trig[80:96, 1] = sin
    return cmat.astype(bf), trig


_NC_CACHE = {}


def kernel(x, meta, mix_pre_g, w_in, q_norm_g, w_uq, kv_norm_g, w_ukv, w_attn_o, conv_w, conv_b,
           conv_ln_g, conv_ln_b, w_conv_o, w_out, mix_post_g, ffn_pre_g, w_ffn_in, w_ffn_out, ffn_post_g):
    f32 = np.float32
    x = np.asarray(x, f32)
    B = x.shape[0]

    def colpack(v, nchunk):
        return np.ascontiguousarray(np.asarray(v, f32).reshape(nchunk, 128).T)

    cols = np.zeros((128, NCOLS), f32)
    cols[:, C_PRE:C_PRE + 8] = colpack(mix_pre_g[0], 8)
    cols[:, C_QG:C_QG + 2] = colpack(q_norm_g[0], 2)
    cols[:, C_KVG:C_KVG + 1] = colpack(kv_norm_g[0], 1)
    cols[:, C_CB:C_CB + 8] = colpack(conv_b[0], 8)
    cols[:, C_LNG:C_LNG + 8] = colpack(conv_ln_g[0], 8)
    cols[:, C_LNB:C_LNB + 8] = colpack(conv_ln_b[0], 8)
    cols[:, C_FPRE:C_FPRE + 8] = colpack(ffn_pre_g[0], 8)
    cw = np.asarray(conv_w[0], f32)
    cols[:, C_CW:] = cw.T.reshape(8, 128, CONV_K).transpose(1, 0, 2).reshape(128, 8 * CONV_K)
    gpost = np.empty((128, 2, D), f32)
    gpost[:, 0, :] = np.asarray(mix_post_g[0], f32)[None, :]
    gpost[:, 1, :] = np.asarray(ffn_post_g[0], f32)[None, :]
    cmat, trig = _host_consts()

    shared = {
        "meta": np.ascontiguousarray(np.asarray(meta, f32)),
        "w_in": np.ascontiguousarray(np.asarray(w_in[0], f32)),
        "w_uq": np.ascontiguousarray(np.asarray(w_uq[0], f32)),
        "w_ukv": np.ascontiguousarray(np.asarray(w_ukv[0], f32)),
        "w_attn_o": np.ascontiguousarray(np.asarray(w_attn_o[0], f32)),
        "w_conv_o": np.ascontiguousarray(np.asarray(w_conv_o[0], f32)),
        "w_out": np.ascontiguousarray(np.asarray(w_out[0], f32)),
        "w_ffn_in": np.ascontiguousarray(np.asarray(w_ffn_in[0], f32)),
        "w_ffn_out": np.ascontiguousarray(np.asarray(w_ffn_out[0], f32)),
        "cols": cols,
        "gpost": gpost,
        "cmat": cmat,
        "trig": trig,
    }
    if "nc" not in _NC_CACHE:
        _NC_CACHE["nc"] = build_nc()
    nc = _NC_CACHE["nc"]
    in_maps = []
    for b in range(B):
        m = dict(shared)
        m["x"] = np.ascontiguousarray(x[b])
        in_maps.append(m)
    res = run_bass_kernel_spmd(nc, in_maps, core_ids=list(range(B)))
    if DEBUG:
        kernel.last_results = res.results
    return np.stack([np.asarray(r["y"], f32) for r in res.results], axis=0)
```

```python
import os
from contextlib import ExitStack

import numpy as np
import ml_dtypes

import concourse.bass as bass
import concourse.mybir as mybir
from concourse.bass_utils import run_bass_kernel_spmd

F32 = mybir.dt.float32
BF16 = mybir.dt.bfloat16
AF = mybir.ActivationFunctionType
ALU = mybir.AluOpType
AX = mybir.AxisListType

D = 1024
SEQ = 2048
NMETA = 16
NCOL = SEQ + NMETA
NH = 16
EPS = 1e-6
DFF = 2816
NFC = DFF // 128
OFF_KV = 256
OFF_KR = 384
OFF_GLU = 416
OFF_GATE = 2464
N_IN = 4512
CONV_K = 31
SCALE = 1.0 / float(np.sqrt(96.0))

C_PRE = 0
C_QG = 8
C_KVG = 10
C_CB = 11
C_LNG = 19
C_LNB = 27
C_FPRE = 35
C_CW = 43
NCOLS = C_CW + 8 * CONV_K

DEBUG = os.environ.get("MK_DEBUG", "") != ""


class Prog:
    def __init__(self):
        self.ops = []
        self.lw = {}
        self.rd = {}
        self.floor = None

    def op(self, eng, fn, r=(), w=(), dma=None):
        oid = len(self.ops)
        raw = set()
        war = set()
        for k in r:
            if k in self.lw:
                raw.add(self.lw[k])
        for k in w:
            if k in self.lw:
                raw.add(self.lw[k])
            war.update(self.rd.get(k, ()))
        if self.floor is not None:
            raw.add(self.floor)
        for k in r:
            self.rd.setdefault(k, []).append(oid)
        for k in w:
            self.lw[k] = oid
            self.rd[k] = []
        deps = set()
        for d in raw | war:
            dop = self.ops[d]
            if dop["dma"] is not None:
                deps.add(d)
            elif dop["eng"] != eng:
                deps.add(d)
            else:
                if eng != "tensor":
                    deps.add(d)
        self.ops.append(dict(eng=eng, fn=fn, deps=deps, dma=dma, has_dep=False))
        return oid

    def barrier(self, eng="vector", fn=None):
        oid = len(self.ops)
        deps = set(self.lw.values())
        for v in self.rd.values():
            deps.update(v)
        if self.floor is not None:
            deps.add(self.floor)
        self.ops.append(dict(eng=eng, fn=fn, deps=deps, dma=None, has_dep=False))
        self.floor = oid
        self.lw = {}
        self.rd = {}
        return oid

    def emit(self, nc, stack):
        ops = self.ops
        for o in ops:
            for d in o["deps"]:
                ops[d]["has_dep"] = True
        engs = ["tensor", "vector", "scalar", "gpsimd", "sync"]
        esem = {e: stack.enter_context(nc.semaphore("s_" + e)) for e in engs}
        ecnt = {e: 0 for e in engs}
        dsem = {}
        dcnt = {}
        for o in ops:
            if not o["has_dep"]:
                o["sig"] = None
                continue
            if o["dma"] is not None:
                k = o["dma"]
                if k not in dsem:
                    dsem[k] = stack.enter_context(nc.semaphore("d_%d" % len(dsem)))
                    dcnt[k] = 0
                dcnt[k] += 16
                o["sig"] = (dsem[k], dcnt[k], 16)
            else:
                assert o["fn"] is not None
                ecnt[o["eng"]] += 1
                o["sig"] = (esem[o["eng"]], ecnt[o["eng"]], 1)
        per = {e: [] for e in engs}
        for o in ops:
            per[o["eng"]].append(o)
        block = stack.enter_context(nc.Block())

        def make(e_name):
            def body(e):
                waited = {}
                for o in per[e_name]:
                    need = {}
                    for d in o["deps"]:
                        sem, val, _ = ops[d]["sig"]
                        key = id(sem)
                        if waited.get(key, 0) >= val:
                            continue
                        if key not in need or need[key][1] < val:
                            need[key] = (sem, val)
                    for key, (sem, val) in need.items():
                        e.wait_ge(sem, val)
                        waited[key] = val
                    if o["fn"] is None:
                        continue
                    ins = o["fn"](e)
                    if o["sig"] is not None:
                        ins.then_inc(o["sig"][0], o["sig"][2])
            return body

        block.tensor(make("tensor"))
        block.vector(make("vector"))
        block.scalar(make("scalar"))
        block.gpsimd(make("gpsimd"))
        block.sync(make("sync"))
        return ecnt, len(dsem)


def build_nc():
    nc = bass.Bass("TRN2", target_bir_lowering=False)
    P = Prog()

    def din(name, shape, dt=F32):
        return nc.dram_tensor(name, list(shape), dt, kind="ExternalInput").ap()

    x = din("x", [SEQ, D])
    meta = din("meta", [NMETA, D])
    w_in = din("w_in", [D, N_IN])
    w_uq = din("w_uq", [256, 1536])
    w_ukv = din("w_ukv", [128, 2048])
    w_ao = din("w_attn_o", [D, D])
    w_co = din("w_conv_o", [D, D])
    w_out = din("w_out", [D, D])
    w_fi = din("w_ffn_in", [D, 2 * DFF])
    w_fo = din("w_ffn_out", [DFF, D])
    cols_d = din("cols", [128, NCOLS])
    gpost_d = din("gpost", [128, 2, D])
    cmat_d = din("cmat", [128, 4, 128], BF16)
    trig_d = din("trig", [128, 2, NCOL])
    y = nc.dram_tensor("y", [SEQ, D], F32, kind="ExternalOutput").ap()
    dbg = {}
    if DEBUG:
        dbg["hT"] = nc.dram_tensor("dbg_hT", [128, 8, NCOL], BF16, kind="ExternalOutput").ap()
        dbg["lat"] = nc.dram_tensor("dbg_lat", [128, 4, NCOL], BF16, kind="ExternalOutput").ap()
        dbg["OT"] = nc.dram_tensor("dbg_OT", [128, 8, SEQ], BF16, kind="ExternalOutput").ap()
        dbg["maT"] = nc.dram_tensor("dbg_maT", [128, 8, SEQ], BF16, kind="ExternalOutput").ap()
        dbg["x1"] = nc.dram_tensor("dbg_x1", [128, 16, D], F32, kind="ExternalOutput").ap()
        dbg["QK"] = nc.dram_tensor("dbg_QK", [128, 2, NCOL], BF16, kind="ExternalOutput").ap()

    stack = ExitStack()
    ARENA_B = 170 * 1024
    cols = stack.enter_context(nc.sbuf_tensor("cols_sb", [128, NCOLS], F32))
    cmat = stack.enter_context(nc.sbuf_tensor("cmat_sb", [128, 4, 128], BF16))
    bscr = stack.enter_context(nc.sbuf_tensor("bscr", [128, 8], F32))
    wsl = stack.enter_context(nc.sbuf_tensor("wsl", [128, 4, 4096], BF16))
    arena = stack.enter_context(nc.sbuf_tensor("arena", [128, ARENA_B // 2], BF16))
    ps = stack.enter_context(nc.psum_tensor("ps", [128, 8, 512], F32))

    ident = cmat[:, 0, :]
    c256 = cmat[:, 1, :]
    c128 = cmat[:, 2, :]
    c1024 = cmat[:, 3, :]

    def view(off_b, shape, dt=BF16):
        n = int(np.prod(shape))
        esz = 2 if dt == BF16 else 4
        assert off_b % 4 == 0 and off_b + n * esz <= ARENA_B, (off_b, shape)
        a = arena[:, off_b // 2: off_b // 2 + n * esz // 2]
        if dt != BF16:
            a = a.bitcast(dt)
        if len(shape) == 2:
            a = a.rearrange("p (a b) -> p a b", a=shape[0])
        elif len(shape) == 3:
            a = a.rearrange("p (a b c) -> p a b c", a=shape[0], b=shape[1])
        return a

    KB = 1024
    A_TRIG = ARENA_B - 2 * NCOL * 4
    trig = view(A_TRIG, [2, NCOL], F32)
    cosT = trig[:, 0, :]
    sinT = trig[:, 1, :]
    bctr = [0]

    def barrier():
        i = bctr[0] % 8
        bctr[0] += 1
        P.barrier("vector", lambda e: e.memset(bscr[:, i:i + 1], 0.0))

    def pst_view(bank):
        return ps[:, bank, :].bitcast(BF16).rearrange("p (k t) -> p k t", k=8)

    def dma_sync(out, in_, r=(), w=(), key=None):
        return P.op("sync", lambda e: e.dma_start(out=out, in_=in_), r=r, w=w, dma=key)

    def dma_cast(out, in_, r=(), w=(), key=None):
        return P.op("gpsimd", lambda e: e.dma_start(out=out, in_=in_), r=r, w=w, dma=key)

    def mm(out, lhsT, rhs, bank, start, stop, r=()):
        return P.op("tensor", lambda e: e.matmul(out, lhsT=lhsT, rhs=rhs, start=start, stop=stop),
                    r=r, w=[("ps", bank)])

    def act(out, in_, func, r=(), w=(), **kw):
        return P.op("scalar", lambda e: e.activation(out=out, in_=in_, func=func, **kw), r=r, w=w)

    def dve(fn, r=(), w=()):
        return P.op("vector", fn, r=r, w=w)

    wctr = [0]

    def wslot():
        s = wctr[0] % 4
        wctr[0] += 1
        return s

    def load_w_slab(src_ap, nk, ncol, slot=None, key=None):
        if slot is None:
            slot = wslot()
        v = wsl[:, slot, 0:nk * ncol].rearrange("p (k n) -> p k n", k=nk)
        dma_cast(v, src_ap.rearrange("(k p) n -> p k n", p=128), w=[("wsl", slot)], key=("wsl", slot))
        return slot, v

    def gated_load(wsrc, gate_col0):
        out = []
        for half in range(2):
            out.append(load_w_slab(wsrc[:, half * 512:(half + 1) * 512], 8, 512))
            out.append(load_w_slab(w_in[:, gate_col0 + half * 512: gate_col0 + (half + 1) * 512], 8, 512))
        return out

    dma_sync(cols[:], cols_d[:, :], w=["cols"], key="c0")
    dma_sync(cmat[:], cmat_d[:, :, :], w=["cmat"], key="c1")
    dma_sync(trig[:, :, :], trig_d[:, :, :], w=["trig"], key="c2")

    def norm_setup(hT, off_scratch, src_tile, gcol, key_out, x1=None):
        o = off_scratch
        if x1 is None:
            xs = view(o, [3, D], F32);  o += 12 * KB
        junk = view(o, [D], BF16);      o += 2 * KB
        xn = view(o, [2, D], BF16);     o += 4 * KB
        ss = view(o, [64], F32);        o += 256
        sd = view(o, [64], F32);        o += 256
        rs = view(o, [64], F32);        o += 256
        gb_src = cols[:, gcol:gcol + 8]

        def prep(i):
            src, rows, c0 = src_tile(i)
            sl = i % 3
            if x1 is None:
                dma_sync(xs[0:rows, sl, :], src, w=[("xs", sl)], key=("xs", sl))
                xin = xs[0:rows, sl, :]
                rkey = ("xs", sl)
            else:
                xin = x1[:, i, :]
                rkey = ("x1", i)
            act(junk[0:rows, :], xin, AF.Square, r=[rkey], w=["junk", ("ss", i)],
                accum_out=ss[0:rows, i:i + 1])
            act(sd[0:rows, i:i + 1], ss[0:rows, i:i + 1], AF.Sqrt, r=[("ss", i)], w=[("sd", i)],
                scale=1.0 / D, bias=EPS)
            dve(lambda e, o=rs[0:rows, i:i + 1], a=sd[0:rows, i:i + 1]: e.reciprocal(out=o, in_=a),
                r=[("sd", i)], w=[("rs", i)])
            dve(lambda e, o=xn[0:rows, i % 2, :], a=xin, s=rs[0:rows, i:i + 1]:
                e.tensor_scalar_mul(out=o, in0=a, scalar1=s),
                r=[rkey, ("rs", i)], w=[("xn", i % 2)])

        def tr(i):
            src, rows, c0 = src_tile(i)
            bank = 6 + (i % 2)
            pv = pst_view(bank)
            for k in range(8):
                P.op("tensor", lambda e, o=pv[:, k, 0:rows], a=xn[0:rows, i % 2, k * 128:(k + 1) * 128],
                     idn=ident[0:rows, 0:rows]: e.transpose(out=o, in_=a, identity=idn),
                     r=[("xn", i % 2), "cmat"], w=[("ps", bank)])
            gb = gb_src.unsqueeze(2).to_broadcast([128, 8, rows])
            dve(lambda e, o=hT[:, :, c0:c0 + rows], a=pv[:, :, 0:rows], g=gb:
                e.tensor_tensor(out=o, in0=a, in1=g, op=ALU.mult),
                r=["cols"], w=[("ps", bank), (key_out, i)])

        def do_tile(i):
            prep(i)
            tr(i)
        do_tile.prep = prep
        do_tile.tr = tr
        return do_tile

    def norm_to_T(hT, off_scratch, src_tile, gcol, key_out, ntiles=17, x1=None):
        do_tile = norm_setup(hT, off_scratch, src_tile, gcol, key_out, x1=x1)
        for i in range(ntiles + 1):
            if i < ntiles:
                do_tile.prep(i)
            if i >= 1:
                do_tile.tr(i - 1)

    def x_tiles(i):
        if i < 16:
            return x[128 * i:128 * i + 128, :], 128, NMETA + 128 * i
        return meta[:, :], NMETA, 0

    def hkeys(c0, n, name="hT"):
        ks = []
        if c0 < NMETA:
            ks.append((name, 16))
        lo = max(c0, NMETA) - NMETA
        hi = c0 + n - NMETA
        if hi > lo:
            for i in range(lo // 128, (hi - 1) // 128 + 1):
                ks.append((name, i))
        return ks

    A_CQ = 0
    A_CKV = 8 * KB
    A_KR = A_CKV + 4160
    A_HT = 17 * KB
    A_SCR = 50 * KB
    A_LAT = 70 * KB
    cqnT = view(A_CQ, [2, SEQ])
    ckvnT = view(A_CKV, [NCOL])
    kropeT = view(A_KR, [NCOL])
    hT = view(A_HT, [8, NCOL])

    lat_slot, wlat = load_w_slab(w_in[:, 0:OFF_GLU], 8, OFF_GLU)
    wuq = view(110 * KB, [2, 1536])
    wukv = view(116 * KB, [2048])
    dma_cast(wuq[:, :, :], w_uq.rearrange("(c p) n -> p c n", p=128), w=["wuq"], key="wuq")
    dma_cast(wukv[:, :], w_ukv[:, :], w=["wukv"], key="wukv")
    wrot = view(A_LAT, [8, 96])
    dve(lambda e: e.tensor_copy(out=wrot[:, :, 0:64], in_=wlat[:, :, 320:384]),
        r=[("wsl", lat_slot)], w=["wrot"])
    dve(lambda e: e.tensor_scalar_mul(out=wrot[:, :, 64:80], in0=wlat[:, :, 400:416], scalar1=-1.0),
        r=[("wsl", lat_slot)], w=["wrot"])
    dve(lambda e: e.tensor_copy(out=wrot[:, :, 80:96], in_=wlat[:, :, 384:400]),
        r=[("wsl", lat_slot)], w=["wrot"])

    norm_to_T(hT, A_SCR, x_tiles, C_PRE, "hT")

    if DEBUG:
        dma_sync(dbg["hT"][:, :, :], hT[:, :, :], r=[("hT", i) for i in range(17)], key="dbg")

    sq = view(A_LAT + 2 * KB, [2, 512])
    sdt = view(A_LAT + 4 * KB, [512], F32)
    rst = view(A_LAT + 6 * KB, [512], F32)
    t1 = view(A_LAT + 8 * KB, [512], F32)
    t2 = view(A_LAT + 10 * KB, [512], F32)

    def rope_evac(out_ap, bank1, bank2, n, tc0, tag, t1, t2):
        dve(lambda e: e.tensor_tensor(out=t1[64:96, 0:n], in0=ps[64:96, bank1, 0:n],
                                      in1=cosT[64:96, tc0:tc0 + n], op=ALU.mult),
            r=["trig"], w=[("ps", bank1), "t1"])
        dve(lambda e: e.tensor_tensor(out=t2[64:96, 0:n], in0=ps[64:96, bank2, 0:n],
                                      in1=sinT[64:96, tc0:tc0 + n], op=ALU.mult),
            r=["trig"], w=[("ps", bank2), "t2"])
        dve(lambda e: e.tensor_tensor(out=out_ap, in0=t1[64:96, 0:n], in1=t2[64:96, 0:n], op=ALU.add),
            r=["t1", "t2"], w=[tag])

    groups = [(NMETA + 512 * g, 512, g) for g in range(4)] + [(0, NMETA, 4)]
    for (c0, n, g) in groups:
        hk = hkeys(c0, n)
        real = g < 4
        targets = []
        if real:
            targets += [(0, 128, 0, 128), (1, 128, 128, 256)]
        targets += [(2, 128, 256, 384), (3, 96, 320, 416)]
        for (bank, M, a, b) in targets:
            for k in range(8):
                mm(ps[0:M, bank, 0:n], wlat[:, k, a:b], hT[:, k, c0:c0 + n], bank, k == 0, k == 7,
                   r=hk + [("wsl", lat_slot)])
        for k in range(8):
            mm(ps[0:96, 4, 0:n], wrot[:, k, :], hT[:, k, c0:c0 + n], 4, k == 0, k == 7, r=hk + ["wrot"])
        if real:
            tcol = c0 - NMETA
            for j in range(2):
                act(sq[:, j, 0:n], ps[:, j, 0:n], AF.Square, w=[("ps", j), ("sq", j)])
            mm(ps[:, 5, 0:n], c256, sq[:, 0, 0:n], 5, True, False, r=[("sq", 0), "cmat"])
            mm(ps[:, 5, 0:n], c256, sq[:, 1, 0:n], 5, False, True, r=[("sq", 1), "cmat"])
            act(sdt[:, 0:n], ps[:, 5, 0:n], AF.Sqrt, w=[("ps", 5), "sdt"], bias=EPS, scale=1.0)
            dve(lambda e, n=n: e.reciprocal(out=rst[:, 0:n], in_=sdt[:, 0:n]), r=["sdt"], w=["rst"])
            for j in range(2):
                dve(lambda e, j=j, n=n, tcol=tcol: e.scalar_tensor_tensor(
                    out=cqnT[:, j, tcol:tcol + n], in0=ps[:, j, 0:n], scalar=cols[:, C_QG + j:C_QG + j + 1],
                    in1=rst[:, 0:n], op0=ALU.mult, op1=ALU.mult),
                    r=["rst", "cols"], w=[("ps", j), ("cqnT", g)])
        act(sq[:, 0, 0:n], ps[:, 2, 0:n], AF.Square, w=[("ps", 2), ("sq", 0)])
        mm(ps[:, 5, 0:n], c128, sq[:, 0, 0:n], 5, True, True, r=[("sq", 0), "cmat"])
        act(sdt[:, 0:n], ps[:, 5, 0:n], AF.Sqrt, w=[("ps", 5), "sdt"], bias=EPS, scale=1.0)
        dve(lambda e, n=n: e.reciprocal(out=rst[:, 0:n], in_=sdt[:, 0:n]), r=["sdt"], w=["rst"])
        dve(lambda e, n=n, c0=c0: e.scalar_tensor_tensor(
            out=ckvnT[:, c0:c0 + n], in0=ps[:, 2, 0:n], scalar=cols[:, C_KVG:C_KVG + 1],
            in1=rst[:, 0:n], op0=ALU.mult, op1=ALU.mult),
            r=["rst", "cols"], w=[("ps", 2), ("ckvnT", g)])
        rope_evac(kropeT[64:96, c0:c0 + n], 3, 4, n, c0, ("kropeT", g), t1, t2)

    if DEBUG:
        dma_sync(dbg["lat"][:, 0:2, NMETA:NCOL], cqnT[:, :, :], r=[("cqnT", g) for g in range(4)], key="dbg")
        dma_sync(dbg["lat"][:, 2, :], ckvnT[:, :], r=[("ckvnT", g) for g in range(5)], key="dbg")
        dma_sync(dbg["lat"][64:96, 3, :], kropeT[64:96, :], r=[("kropeT", g) for g in range(5)], key="dbg")

    barrier()

    preC = gated_load(w_ao, OFF_GATE)
    B0 = 50 * KB
    wq2 = view(B0, [2, 1536]);            B0 += 6 * KB
    Vaug = view(B0, [2, 17, 128]);        B0 += 8704
    QT = view(B0, [2, SEQ]);              B0 += 8 * KB
    KT = view(B0, [2, NCOL]);             B0 += 8256
    PT = view(B0, [4, 2, 512]);           B0 += 8 * KB
    PTm = view(B0, [2, 512]);             B0 += 2 * KB
    rc = view(B0, [2, 512], F32);         B0 += 4 * KB
    tb1 = view(B0, [512], F32);           B0 += 2 * KB
    tb2 = view(B0, [512], F32);           B0 += 2 * KB
    assert B0 <= 110 * KB, B0
    A_OT = 121 * KB
    OT = view(A_OT, [8, SEQ])

    dve(lambda e: e.tensor_copy(out=wq2[:, :, :], in_=wuq[:, :, :]), r=["wuq"], w=["wq2"])
    wuq4 = wuq.rearrange("p c (h d) -> p c h d", h=NH)
    wq24 = wq2.rearrange("p c (h d) -> p c h d", h=NH)
    for c in range(2):
        dve(lambda e, c=c: e.tensor_scalar_mul(out=wq24[:, c, :, 64:80], in0=wuq4[:, c, :, 80:96], scalar1=-1.0),
            r=["wuq"], w=["wq2"])
        dve(lambda e, c=c: e.tensor_copy(out=wq24[:, c, :, 80:96], in_=wuq4[:, c, :, 64:80]),
            r=["wuq"], w=["wq2"])
    dve(lambda e: e.memset(Vaug[:, 0, :, 64:128], 1.0), w=[("Vaug", 0)])
    dve(lambda e: e.memset(Vaug[:, 1, :, 0:64], 1.0), w=[("Vaug", 1)])
    for b in range(2):
        dve(lambda e, b=b: e.tensor_copy(out=KT[64:96, b, :], in_=kropeT[64:96, :]), w=[("KTr", b)])

    kgroups = [(NMETA + 512 * g, 512) for g in range(4)] + [(0, NMETA)]

    def proj_chunks(h):
        b = h % 2
        voff = 0 if b == 0 else 64
        chunks = []

        wvh = wukv[:, h * 128 + 64:h * 128 + 128]
        for vi, (j_lo, j_hi) in enumerate([(0, 8), (8, 16), (16, 17)]):
            def c_v(vi=vi, j_lo=j_lo, j_hi=j_hi):
                bank = 6 + (vi % 2)
                nt = j_hi - j_lo
                rows = 128 if j_lo < 16 else NMETA
                for j in range(j_lo, j_hi):
                    kc0 = NMETA + 128 * j if j < 16 else 0
                    mm(ps[0:rows, bank, (j - j_lo) * 64:(j - j_lo + 1) * 64], ckvnT[:, kc0:kc0 + rows], wvh, bank,
                       True, True, r=["wukv"])
                src = ps[0:rows, bank, 0:nt * 64].rearrange("p (t d) -> p t d", t=nt)
                dve(lambda e: e.tensor_copy(out=Vaug[0:rows, b, j_lo:j_hi, voff:voff + 64], in_=src),
                    w=[("ps", bank), ("Vaug", b)])
            chunks.append(c_v)
        for gi, (c0, n) in enumerate(kgroups):
            def c_k(gi=gi, c0=c0, n=n):
                bank = 6 + (gi % 2)
                mm(ps[0:64, bank, 0:n], wukv[:, h * 128:h * 128 + 64], ckvnT[:, c0:c0 + n], bank, True, True,
                   r=["wukv"])
                dve(lambda e: e.tensor_copy(out=KT[0:64, b, c0:c0 + n], in_=ps[0:64, bank, 0:n]),
                    w=[("ps", bank), ("KT", b)])
            chunks.append(c_k)
        for g in range(4):
            def c_q(g=g):
                for c in range(2):
                    mm(ps[0:96, 6, :], wuq[:, c, h * 96:(h + 1) * 96], cqnT[:, c, g * 512:(g + 1) * 512], 6,
                       c == 0, c == 1, r=["wuq"])
                for c in range(2):
                    mm(ps[0:96, 7, :], wq2[:, c, h * 96:(h + 1) * 96], cqnT[:, c, g * 512:(g + 1) * 512], 7,
                       c == 0, c == 1, r=["wq2"])
                dve(lambda e: e.tensor_copy(out=QT[0:64, b, g * 512:(g + 1) * 512], in_=ps[0:64, 6, :]),
                    w=[("ps", 6), ("QT", b, g)])
                rope_evac(QT[64:96, b, g * 512:(g + 1) * 512], 6, 7, 512, NMETA + g * 512, ("QTr", b, g), tb1, tb2)
            chunks.append(c_q)
        return chunks

    pair_ctr = [0]
    o_ctr = [0]
    gstate = {}

    def stage1(t):
        h, g = t["h"], t["g"]
        b = h % 2
        qk = [("QT", b, g), ("QTr", b, g)]
        kk = [("KT", b), ("KTr", b)]
        q0 = g * 512
        pc = pair_ctr[0]
        pair_ctr[0] += 1
        sb = 2 * (pc % 2)
        t["sb"] = sb
        if t["kind"] == "meta":
            t["msl"] = pc % 2
            ob = 4 + (o_ctr[0] % 2)
            gstate[(h, g)] = (ob, o_ctr[0] % 2)
            o_ctr[0] += 1
            mm(ps[0:16, sb, :], KT[0:96, b, 0:NMETA], QT[0:96, b, q0:q0 + 512], sb, True, True, r=qk + kk)
        else:
            t["slot"] = pc % 4
            los = []
            for a in range(2):
                j = t["j0"] + a
                lo = max(0, j - 4 * g) * 128
                los.append(lo)
                kc = NMETA + 128 * j
                mm(ps[:, sb + a, lo:512], KT[0:96, b, kc:kc + 128], QT[0:96, b, q0 + lo:q0 + 512], sb + a,
                   True, True, r=qk + kk)
            t["los"] = los

    def stage23(t):
        h, g = t["h"], t["g"]
        b = h % 2
        q0 = g * 512
        sb = t["sb"]
        ob, rsl = gstate[(h, g)]
        nkt = 4 * g + 4
        if t["kind"] == "meta":
            msl = t["msl"]
            act(PTm[0:16, msl, :], ps[0:16, sb, :], AF.Exp, w=[("ps", sb), ("PTm", msl)], scale=SCALE)
            mm(ps[:, ob, :], Vaug[0:16, b, 16, :], PTm[0:16, msl, :], ob, True, False,
               r=[("PTm", msl), ("Vaug", b)])
            return
        slot, los, j0 = t["slot"], t["los"], t["j0"]
        if los[0] == 0 and los[1] == 0:
            act(PT[:, slot, :, :], ps[:, sb:sb + 2, :], AF.Exp,
                w=[("ps", sb), ("ps", sb + 1), ("PT", slot, 0), ("PT", slot, 1)], scale=SCALE)
        else:
            for a in range(2):
                lo = los[a]
                act(PT[:, slot, a, lo:512], ps[:, sb + a, lo:512], AF.Exp,
                    w=[("ps", sb + a), ("PT", slot, a)], scale=SCALE)
        for a in range(2):
            j = j0 + a
            lo = los[a]
            if j >= 4 * g:
                dve(lambda e, a=a, lo=lo: e.memset(PT[64:128, slot, a, lo:lo + 64], 0.0), w=[("PT", slot, a)])
            mm(ps[:, ob, lo:512], Vaug[:, b, j, :], PT[:, slot, a, lo:512], ob, False, j == nkt - 1,
               r=[("PT", slot, a), ("Vaug", b)])
        if j0 + 2 == nkt:
            if b == 0:
                dlo, slo = 0, 64
            else:
                dlo, slo = 64, 0
            act(rc[dlo:dlo + 64, rsl, :], ps[slo:slo + 64, ob, :], AF.Ln, w=[("ps", ob), ("rc", rsl)])
            act(rc[dlo:dlo + 64, rsl, :], rc[dlo:dlo + 64, rsl, :], AF.Exp, r=[("rc", rsl)], w=[("rc", rsl)],
                scale=-1.0)
            dve(lambda e: e.tensor_tensor(out=OT[dlo:dlo + 64, h // 2, q0:q0 + 512], in0=ps[dlo:dlo + 64, ob, :],
                                          in1=rc[dlo:dlo + 64, rsl, :], op=ALU.mult),
                r=[("rc", rsl)], w=[("ps", ob), ("OT", h // 2, g)])

    for ch in proj_chunks(0):
        ch()
    if DEBUG:
        dma_sync(dbg["QK"][0:96, 0, NMETA:NCOL], QT[0:96, 0, :], r=[("QT", 0, g) for g in range(4)] +
                 [("QTr", 0, g) for g in range(4)], key="dbg")
        dma_sync(dbg["QK"][0:96, 1, :], KT[0:96, 0, :], r=[("KT", 0), ("KTr", 0)], key="dbg")
    tasks = []
    for h in range(NH):
        pend = proj_chunks(h + 1) if h + 1 < NH else []
        ht = []
        for g in range(4):
            ht.append(dict(kind="meta", h=h, g=g))
            for j0 in range(0, 4 * g + 4, 2):
                ht.append(dict(kind="pair", h=h, g=g, j0=j0))
        for ti, t in enumerate(ht):
            t["extra"] = []
            if ti >= 2 and ti % 2 == 0 and pend:
                t["extra"].append(pend.pop(0))
        while pend:
            ht[-1]["extra"].append(pend.pop(0))
        tasks += ht
    for i in range(len(tasks) + 1):
        if i < len(tasks):
            stage1(tasks[i])
        if i >= 1:
            stage23(tasks[i - 1])
            for ch in tasks[i - 1]["extra"]:
                ch()

    if DEBUG:
        dma_sync(dbg["OT"][:, :, :], OT[:, :, :], r=[("OT", k, g) for k in range(8) for g in range(4)], key="dbg")

    barrier()

    maT = view(50 * KB, [8, SEQ])
    sg = view(0, [2, 512])
    tmpb = view(2 * KB, [2, 512])

    def gated_proj(pre, src_rhs, src_keys, accumulate, after_half0=None):
        cnt = 0
        for half in range(2):
            if half == 1 and after_half0 is not None:
                after_half0()
            s_w, wv = pre[2 * half]
            s_g, gv = pre[2 * half + 1]
            for mi in range(4):
                m = half * 4 + mi
                for g in range(4):
                    by = 2 * (cnt % 2)
                    bg = by + 1
                    cnt += 1
                    for k in range(8):
                        mm(ps[:, by, :], wv[:, k, mi * 128:(mi + 1) * 128], src_rhs(k, g), by, k == 0, k == 7,
                           r=[("wsl", s_w)] + src_keys(k, g))
                    for k in range(8):
                        mm(ps[:, bg, :], gv[:, k, mi * 128:(mi + 1) * 128],
                           hT[:, k, NMETA + g * 512:NMETA + (g + 1) * 512], bg, k == 0, k == 7,
                           r=[("wsl", s_g)] + hkeys(NMETA + g * 512, 512))
                    sl = cnt % 2
                    act(sg[:, sl, :], ps[:, bg, :], AF.Tanh, w=[("ps", bg), ("sg", sl)], scale=0.5)
                    if not accumulate:
                        dve(lambda e, m=m, g=g, by=by, sl=sl: e.scalar_tensor_tensor(
                            out=maT[:, m, g * 512:(g + 1) * 512], in0=sg[:, sl, :], scalar=1.0, in1=ps[:, by, :],
                            op0=ALU.add, op1=ALU.mult),
                            r=[("sg", sl)], w=[("ps", by), ("maT", m, g)])
                    else:
                        dve(lambda e, by=by, sl=sl: e.scalar_tensor_tensor(
                            out=tmpb[:, sl, :], in0=sg[:, sl, :], scalar=1.0, in1=ps[:, by, :],
                            op0=ALU.add, op1=ALU.mult),
                            r=[("sg", sl)], w=[("ps", by), ("tmpb", sl)])
                        dve(lambda e, m=m, g=g, sl=sl: e.tensor_tensor(
                            out=maT[:, m, g * 512:(g + 1) * 512], in0=maT[:, m, g * 512:(g + 1) * 512],
                            in1=tmpb[:, sl, :], op=ALU.add),
                            r=[("tmpb", sl)], w=[("maT", m, g)])

    glu_slabs = {}

    def load_glu(half):
        glu_slabs[("a", half)] = load_w_slab(w_in[:, OFF_GLU + half * 512:OFF_GLU + (half + 1) * 512], 8, 512)
        glu_slabs[("g", half)] = load_w_slab(w_in[:, OFF_GLU + D + half * 512:OFF_GLU + D + (half + 1) * 512], 8, 512)

    gated_proj(preC, lambda k, g: OT[:, k, g * 512:(g + 1) * 512], lambda k, g: [("OT", k, g)],
               False, after_half0=lambda: load_glu(0))
    load_glu(1)

    barrier()

    UP = 30 + NCOL
    upad = view(82 * KB, [8, UP + 2])
    D1 = 82 * KB + 33536
    diag = view(D1, [2, CONV_K, 128]);   D1 += 2 * CONV_K * 256
    tn = view(D1, [2, 512], F32);        D1 += 4 * KB
    sgd = view(D1, [2, 512]);            D1 += 2 * KB
    assert D1 <= 138 * KB, D1
    D2 = 4 * KB
    ysq = view(D2, [2, 512]);            D2 += 2 * KB
    mean_sb = view(D2, [512], F32);      D2 += 2 * KB
    var_sb = view(D2, [512], F32);       D2 += 2 * KB
    rstd_sb = view(D2, [512], F32);      D2 += 2 * KB
    nmr_sb = view(D2, [512], F32);       D2 += 2 * KB
    assert D2 <= 17 * KB
    yc = view(138 * KB, [8, SEQ])

    for m in range(8):
        dve(lambda e, m=m: e.memset(upad[:, m, 0:30], 0.0), w=[("upad", m, 5)])
    cnt = 0
    for m in range(8):
        half, mi = m // 4, m % 4
        s_a, av = glu_slabs[("a", half)]
        s_g, gv = glu_slabs[("g", half)]
        for (c0, n, g) in groups:
            ba = 2 * (cnt % 2)
            bg = ba + 1
            sl = cnt % 2
            cnt += 1
            hk = hkeys(c0, n)
            for k in range(8):
                mm(ps[:, ba, 0:n], av[:, k, mi * 128:(mi + 1) * 128], hT[:, k, c0:c0 + n], ba, k == 0, k == 7,
                   r=hk + [("wsl", s_a)])
            for k in range(8):
                mm(ps[:, bg, 0:n], gv[:, k, mi * 128:(mi + 1) * 128], hT[:, k, c0:c0 + n], bg, k == 0, k == 7,
                   r=hk + [("wsl", s_g)])
            act(sgd[:, sl, 0:n], ps[:, bg, 0:n], AF.Sigmoid, w=[("ps", bg), ("sgd", sl)])
            dve(lambda e, m=m, c0=c0, n=n, ba=ba, sl=sl: e.tensor_tensor(
                out=upad[:, m, 30 + c0:30 + c0 + n], in0=ps[:, ba, 0:n], in1=sgd[:, sl, 0:n], op=ALU.mult),
                r=[("sgd", sl)], w=[("ps", ba), ("upad", m, g)])
    preD = gated_load(w_co, OFF_GATE + D)
    cnt = 0
    for m in range(8):
        db = m % 2
        for k in range(CONV_K):
            dve(lambda e, m=m, k=k, db=db: e.tensor_scalar_mul(
                out=diag[:, db, k, :], in0=ident, scalar1=cols[:, C_CW + m * CONV_K + k:C_CW + m * CONV_K + k + 1]),
                r=["cols", "cmat"], w=[("diag", db)])
        for g in range(4):
            bank = 4 + (cnt % 2)
            cnt += 1
            p0 = NMETA + g * 512
            for k in range(CONV_K):
                mm(ps[:, bank, :], diag[:, db, k, :], upad[:, m, p0 + k:p0 + k + 512], bank, k == 0, k == CONV_K - 1,
                   r=[("diag", db)] + [("upad", m, gg) for gg in range(6)])
            act(yc[:, m, g * 512:(g + 1) * 512], ps[:, bank, :], AF.Identity, r=["cols"],
                w=[("ps", bank), ("yc", m, g)], bias=cols[:, C_CB + m:C_CB + m + 1], scale=1.0)
    def ln_stats(g):
        gs = slice(g * 512, (g + 1) * 512)
        for m in range(8):
            sl = m % 2
            dve(lambda e, m=m, sl=sl: e.tensor_tensor(out=ysq[:, sl, :], in0=yc[:, m, gs], in1=yc[:, m, gs],
                                                      op=ALU.mult),
                r=[("yc", m, g)], w=[("ysq", sl)])
            mm(ps[:, 6, :], c1024, yc[:, m, gs], 6, m == 0, m == 7, r=[("yc", m, g), "cmat"])
            mm(ps[:, 7, :], c1024, ysq[:, sl, :], 7, m == 0, m == 7, r=[("ysq", sl), "cmat"])
        dve(lambda e: e.tensor_copy(out=mean_sb[:, :], in_=ps[:, 6, :]), w=[("ps", 6), "mean"])
        dve(lambda e: e.tensor_tensor(out=var_sb[:, :], in0=mean_sb[:, :], in1=mean_sb[:, :], op=ALU.mult),
            r=["mean"], w=["var"])
        dve(lambda e: e.tensor_tensor(out=var_sb[:, :], in0=ps[:, 7, :], in1=var_sb[:, :], op=ALU.subtract),
            r=["var"], w=[("ps", 7), "var"])
        act(var_sb[:, :], var_sb[:, :], AF.Sqrt, r=["var"], w=["var"], bias=EPS, scale=1.0)
        dve(lambda e: e.reciprocal(out=rstd_sb[:, :], in_=var_sb[:, :]), r=["var"], w=["rstd"])
        dve(lambda e: e.tensor_tensor(out=nmr_sb[:, :], in0=mean_sb[:, :], in1=rstd_sb[:, :], op=ALU.mult),
            r=["mean", "rstd"], w=["nmr"])

    def ln_apply(g, m):
        gs = slice(g * 512, (g + 1) * 512)
        sl = m % 2
        dve(lambda e: e.tensor_tensor(out=tn[:, sl, :], in0=yc[:, m, gs], in1=rstd_sb[:, :], op=ALU.mult),
            r=[("yc", m, g), "rstd"], w=[("tn", sl)])
        dve(lambda e: e.tensor_tensor(out=tn[:, sl, :], in0=tn[:, sl, :], in1=nmr_sb[:, :], op=ALU.subtract),
            r=["nmr"], w=[("tn", sl)])
        act(yc[:, m, gs], tn[:, sl, :], AF.Silu, r=[("tn", sl), "cols"], w=[("yc", m, g)],
            scale=cols[:, C_LNG + m:C_LNG + m + 1], bias=cols[:, C_LNB + m:C_LNB + m + 1])

    gcnt = [0]

    def gated_unit(g, m):
        half, mi = m // 4, m % 4
        s_w, wv = preD[2 * half]
        s_g, gv = preD[2 * half + 1]
        by = 2 * (gcnt[0] % 2)
        bg = by + 1
        gcnt[0] += 1
        sl = gcnt[0] % 2
        for k in range(8):
            mm(ps[:, by, :], wv[:, k, mi * 128:(mi + 1) * 128], yc[:, k, g * 512:(g + 1) * 512], by, k == 0, k == 7,
               r=[("wsl", s_w), ("yc", k, g)])
        for k in range(8):
            mm(ps[:, bg, :], gv[:, k, mi * 128:(mi + 1) * 128],
               hT[:, k, NMETA + g * 512:NMETA + (g + 1) * 512], bg, k == 0, k == 7,
               r=[("wsl", s_g)] + hkeys(NMETA + g * 512, 512))
        act(sg[:, sl, :], ps[:, bg, :], AF.Tanh, w=[("ps", bg), ("sg", sl)], scale=0.5)
        dve(lambda e: e.scalar_tensor_tensor(out=tmpb[:, sl, :], in0=sg[:, sl, :], scalar=1.0, in1=ps[:, by, :],
                                             op0=ALU.add, op1=ALU.mult),
            r=[("sg", sl)], w=[("ps", by), ("tmpb", sl)])
        dve(lambda e: e.tensor_tensor(out=maT[:, m, g * 512:(g + 1) * 512], in0=maT[:, m, g * 512:(g + 1) * 512],
                                      in1=tmpb[:, sl, :], op=ALU.add),
            r=[("tmpb", sl)], w=[("maT", m, g)])

    ln_stats(0)
    for m in range(8):
        ln_apply(0, m)
    for g in range(4):
        if g < 3:
            ln_stats(g + 1)
        for m in range(8):
            gated_unit(g, m)
            if g < 3:
                ln_apply(g + 1, m)
            if g == 3 and m == 3:
                wo_slabs = [load_w_slab(w_out[:, nh * 512:(nh + 1) * 512], 8, 512) for nh in range(2)]

    if DEBUG:
        dma_sync(dbg["maT"][:, :, :], maT[:, :, :], r=[("maT", m, g) for m in range(8) for g in range(4)], key="dbg")

    barrier()

    h2T = view(0, [8, SEQ])
    gpost = view(32 * KB, [2, D], F32)
    xs2 = view(40 * KB, [2, D], F32)
    E_SCR = 82 * KB
    junk = view(E_SCR, [D])
    ssz = view(E_SCR + 2 * KB, [64], F32)
    sdz = view(E_SCR + 2 * KB + 256, [64], F32)
    rz = view(E_SCR + 2 * KB + 512, [64], F32)
    tmpf = view(E_SCR + 3 * KB, [2, 512], F32)
    x1 = view(106 * KB, [16, D], F32)
    ffn_norm_tile = norm_setup(h2T, E_SCR + 7 * KB, lambda i: (None, 128, 128 * i), C_FPRE, "h2T", x1=x1)

    def ffn_pair_load(cp):
        slot = wslot()
        gv = wsl[:, slot, 0:2048].rearrange("p (k n) -> p k n", k=8)
        uv = wsl[:, slot, 2048:4096].rearrange("p (k n) -> p k n", k=8)
        dma_cast(gv, w_fi[:, cp * 256:(cp + 1) * 256].rearrange("(k p) n -> p k n", p=128),
                 w=[("wslg", slot)], key=("wslg", slot))
        dma_cast(uv, w_fi[:, DFF + cp * 256:DFF + (cp + 1) * 256].rearrange("(k p) n -> p k n", p=128),
                 w=[("wslu", slot)], key=("wslu", slot))
        return slot, gv, uv

    pre_pairs = [ffn_pair_load(0), ffn_pair_load(1)]
    dma_sync(gpost[:, :, :], gpost_d[:, :, :], w=["gpost"], key="c3")
    def e_mm(i):
        dma_sync(xs2[:, i % 2, :], x[128 * i:128 * i + 128, :], w=[("xs2", i % 2)], key=("xs2", i % 2))
        for nh in range(2):
            s_w, wv = wo_slabs[nh]
            bank = 2 * (i % 3) + nh
            for k in range(8):
                mm(ps[:, bank, :], maT[:, k, i * 128:(i + 1) * 128], wv[:, k, :], bank, k == 0, k == 7,
                   r=[("wsl", s_w)] + [("maT", k, i // 4)])

    def e_evac(i):
        sl = i % 2
        for nh in range(2):
            bank = 2 * (i % 3) + nh
            act(junk[:, 0:512], ps[:, bank, :], AF.Square, w=[("ps", bank), "junk", ("ssz", i, nh)],
                accum_out=ssz[:, 2 * i + nh:2 * i + nh + 1], scale=0.5)
        dve(lambda e: e.tensor_tensor(out=sdz[:, i:i + 1], in0=ssz[:, 2 * i:2 * i + 1],
                                      in1=ssz[:, 2 * i + 1:2 * i + 2], op=ALU.add),
            r=[("ssz", i, 0), ("ssz", i, 1)], w=[("sdz", i)])
        act(sdz[:, i:i + 1], sdz[:, i:i + 1], AF.Sqrt, r=[("sdz", i)], w=[("sdz", i)], scale=1.0 / D, bias=EPS)
        dve(lambda e: e.reciprocal(out=rz[:, i:i + 1], in_=sdz[:, i:i + 1]), r=[("sdz", i)], w=[("rz", i)])
        dve(lambda e: e.tensor_scalar_mul(out=rz[:, i:i + 1], in0=rz[:, i:i + 1], scalar1=0.5),
            r=[("rz", i)], w=[("rz", i)])
        for nh in range(2):
            bank = 2 * (i % 3) + nh
            dve(lambda e, nh=nh, bank=bank: e.scalar_tensor_tensor(
                out=tmpf[:, nh, :], in0=ps[:, bank, :], scalar=rz[:, i:i + 1],
                in1=gpost[:, 0, nh * 512:(nh + 1) * 512], op0=ALU.mult, op1=ALU.mult),
                r=[("rz", i), "gpost"], w=[("ps", bank), ("tmpf", nh)])
            dve(lambda e, nh=nh: e.tensor_tensor(
                out=x1[:, i, nh * 512:(nh + 1) * 512], in0=tmpf[:, nh, :], in1=xs2[:, sl, nh * 512:(nh + 1) * 512],
                op=ALU.add),
                r=[("tmpf", nh), ("xs2", sl)], w=[("x1", i)])

    for it in range(16 + 2):
        if it < 16:
            e_mm(it)
        if 1 <= it <= 16:
            e_evac(it - 1)
            ffn_norm_tile.prep(it - 1)
        if 2 <= it:
            ffn_norm_tile.tr(it - 2)
    if DEBUG:
        dma_sync(dbg["x1"][:, :, :], x1[:, :, :], r=[("x1", i) for i in range(16)], key="dbg")

    barrier()

    F0 = 32 * KB
    actT = view(F0, [NFC, 512]);                 F0 += NFC * 1024
    wfo = view(F0, [2, 11, 512]);                F0 += 2 * 11 * 1024
    slu = view(F0, [2, 512], F32);               F0 += 4 * KB
    fb0 = view(F0, [4, 512], F32);               F0 += 8 * KB
    tmpo = view(F0, [2, 512], F32);              F0 += 4 * KB
    junkf = view(F0, [512]);                     F0 += 1 * KB
    ssf = view(F0, [64], F32);                   F0 += 256
    sdf = view(F0, [64], F32);                   F0 += 256
    rf = view(F0, [64], F32);                    F0 += 256
    assert F0 <= 98 * KB, F0
    gpost_f = view(98 * KB, [2, D], F32)
    dma_sync(gpost_f[:, :, :], gpost_d[:, :, :], w=["gpostf"], key="c4")

    wfo_ctr = [0]
    for q in range(4):
        qs = slice(q * 512, (q + 1) * 512)
        hk = [("h2T", i) for i in range(4 * q, 4 * q + 4)]
        cnt = 0
        for cp in range(NFC // 2):
            if q == 0 and cp < 2:
                slot, gv, uv = pre_pairs[cp]
            else:
                slot, gv, uv = ffn_pair_load(cp)
            for ci in range(2):
                c = cp * 2 + ci
                bg = 4 + 2 * (cnt % 2)
                bu = bg + 1
                sl = cnt % 2
                cnt += 1
                for k in range(8):
                    mm(ps[:, bg, :], gv[:, k, ci * 128:(ci + 1) * 128], h2T[:, k, qs], bg, k == 0, k == 7,
                       r=hk + [("wslg", slot)])
                for k in range(8):
                    mm(ps[:, bu, :], uv[:, k, ci * 128:(ci + 1) * 128], h2T[:, k, qs], bu, k == 0, k == 7,
                       r=hk + [("wslu", slot)])
                act(slu[:, sl, :], ps[:, bg, :], AF.Silu, w=[("ps", bg), ("slu", sl)])
                dve(lambda e, c=c, bu=bu, sl=sl: e.tensor_tensor(out=actT[:, c, :], in0=ps[:, bu, :],
                                                                 in1=slu[:, sl, :], op=ALU.mult),
                    r=[("slu", sl)], w=[("ps", bu), ("actT", c)])
        for nh in range(2):
            for ch in range(2):
                wb = wfo_ctr[0] % 2
                wfo_ctr[0] += 1
                dma_cast(wfo[:, wb, :, :],
                         w_fo[ch * 1408:(ch + 1) * 1408, nh * 512:(nh + 1) * 512].rearrange("(k p) n -> p k n", p=128),
                         w=[("wfo", wb)], key=("wfo", wb))
                for it in range(4):
                    for cc in range(11):
                        c = ch * 11 + cc
                        mm(ps[:, it, :], actT[:, c, it * 128:(it + 1) * 128], wfo[:, wb, cc, :], it,
                           c == 0, c == NFC - 1, r=[("actT", c), ("wfo", wb)])
            for it in range(4):
                i = 4 * q + it
                act(junkf[:, :], ps[:, it, :], AF.Square, w=[("ps", it), "junkf", ("ssf", i, nh)],
                    accum_out=ssf[:, 2 * i + nh:2 * i + nh + 1])
                if nh == 0:
                    dve(lambda e, it=it: e.tensor_copy(out=fb0[:, it, :], in_=ps[:, it, :]),
                        w=[("ps", it), ("fb0", it)])
                else:
                    dve(lambda e, i=i: e.tensor_tensor(out=sdf[:, i:i + 1], in0=ssf[:, 2 * i:2 * i + 1],
                                                       in1=ssf[:, 2 * i + 1:2 * i + 2], op=ALU.add),
                        r=[("ssf", i, 0), ("ssf", i, 1)], w=[("sdf", i)])
                    act(sdf[:, i:i + 1], sdf[:, i:i + 1], AF.Sqrt, r=[("sdf", i)], w=[("sdf", i)],
                        scale=1.0 / D, bias=EPS)
                    dve(lambda e, i=i: e.reciprocal(out=rf[:, i:i + 1], in_=sdf[:, i:i + 1]),
                        r=[("sdf", i)], w=[("rf", i)])
                    for hh in range(2):
                        src = fb0[:, it, :] if hh == 0 else ps[:, it, :]
                        wk = [("tmpo", hh)] + ([("ps", it)] if hh == 1 else [])
                        rk = [("rf", i), "gpostf"] + ([("fb0", it)] if hh == 0 else [])
                        dve(lambda e, i=i, hh=hh, src=src: e.scalar_tensor_tensor(
                            out=tmpo[:, hh, :], in0=src, scalar=rf[:, i:i + 1],
                            in1=gpost_f[:, 1, hh * 512:(hh + 1) * 512], op0=ALU.mult, op1=ALU.mult),
                            r=rk, w=wk)
                        dve(lambda e, i=i, hh=hh: e.tensor_tensor(
                            out=x1[:, i, hh * 512:(hh + 1) * 512], in0=x1[:, i, hh * 512:(hh + 1) * 512],
                            in1=tmpo[:, hh, :], op=ALU.add),
                            r=[("tmpo", hh)], w=[("x1o", i)])
                    dma_sync(y[128 * i:128 * i + 128, :], x1[:, i, :], r=[("x1o", i)], w=[("yout", i)],
                             key=("yout", i % 4))

    if DEBUG:
        P.barrier(eng="sync")
    else:
        P.op("sync", None, r=[("yout", i) for i in range(16)])

    ecnt, nd = P.emit(nc, stack)
    stack.close()
    return nc


def _host_consts():
    bf = ml_dtypes.bfloat16
    cmat = np.zeros((128, 4, 128), np.float32)
    cmat[:, 0, :] = np.eye(128, dtype=np.float32)
    cmat[:, 1, :] = 1.0 / 256.0
    cmat[:, 2, :] = 1.0 / 128.0
    cmat[:, 3, :] = 1.0 / 1024.0
    pos = np.arange(NCOL, dtype=np.float32)
    inv = (np.float32(10000.0) ** (-np.arange(0, 32, 2, dtype=np.float32) / np.float32(32))).astype(np.float32)
    ang = pos[:, None] * inv[None, :]
    cos = np.cos(ang).astype(np.float32).T
    sin = np.sin(ang).astype(np.float32).T
    trig = np.zeros((128, 2, NCOL), np.float32)
    trig[64:80, 0] = cos
    trig[80:96, 0] = cos
    trig[64:80, 1] = sin
    trig[80:96, 1] = sin
    return cmat.astype(bf), trig


_NC_CACHE = {}


def kernel(x, meta, mix_pre_g, w_in, q_norm_g, w_uq, kv_norm_g, w_ukv, w_attn_o, conv_w, conv_b,
           conv_ln_g, conv_ln_b, w_conv_o, w_out, mix_post_g, ffn_pre_g, w_ffn_in, w_ffn_out, ffn_post_g):
    f32 = np.float32
    x = np.asarray(x, f32)
    B = x.shape[0]

    def colpack(v, nchunk):
        return np.ascontiguousarray(np.asarray(v, f32).reshape(nchunk, 128).T)

    cols = np.zeros((128, NCOLS), f32)
    cols[:, C_PRE:C_PRE + 8] = colpack(mix_pre_g[0], 8)
    cols[:, C_QG:C_QG + 2] = colpack(q_norm_g[0], 2)
    cols[:, C_KVG:C_KVG + 1] = colpack(kv_norm_g[0], 1)
    cols[:, C_CB:C_CB + 8] = colpack(conv_b[0], 8)
    cols[:, C_LNG:C_LNG + 8] = colpack(conv_ln_g[0], 8)
    cols[:, C_LNB:C_LNB + 8] = colpack(conv_ln_b[0], 8)
    cols[:, C_FPRE:C_FPRE + 8] = colpack(ffn_pre_g[0], 8)
    cw = np.asarray(conv_w[0], f32)
    cols[:, C_CW:] = cw.T.reshape(8, 128, CONV_K).transpose(1, 0, 2).reshape(128, 8 * CONV_K)
    gpost = np.empty((128, 2, D), f32)
    gpost[:, 0, :] = np.asarray(mix_post_g[0], f32)[None, :]
    gpost[:, 1, :] = np.asarray(ffn_post_g[0], f32)[None, :]
    cmat, trig = _host_consts()

    shared = {
        "meta": np.ascontiguousarray(np.asarray(meta, f32)),
        "w_in": np.ascontiguousarray(np.asarray(w_in[0], f32)),
        "w_uq": np.ascontiguousarray(np.asarray(w_uq[0], f32)),
        "w_ukv": np.ascontiguousarray(np.asarray(w_ukv[0], f32)),
        "w_attn_o": np.ascontiguousarray(np.asarray(w_attn_o[0], f32)),
        "w_conv_o": np.ascontiguousarray(np.asarray(w_conv_o[0], f32)),
        "w_out": np.ascontiguousarray(np.asarray(w_out[0], f32)),
        "w_ffn_in": np.ascontiguousarray(np.asarray(w_ffn_in[0], f32)),
        "w_ffn_out": np.ascontiguousarray(np.asarray(w_ffn_out[0], f32)),
        "cols": cols,
        "gpost": gpost,
        "cmat": cmat,
        "trig": trig,
    }
    if "nc" not in _NC_CACHE:
        _NC_CACHE["nc"] = build_nc()
    nc = _NC_CACHE["nc"]
    in_maps = []
    for b in range(B):
        m = dict(shared)
        m["x"] = np.ascontiguousarray(x[b])
        in_maps.append(m)
    res = run_bass_kernel_spmd(nc, in_maps, core_ids=list(range(B)))
    if DEBUG:
        kernel.last_results = res.results
    return np.stack([np.asarray(r["y"], f32) for r in res.results], axis=0)
```
